# Optimizing a Trainium2 kernel written in Bass

```python
import math
import jax, jax.numpy as jnp
from jax import lax
import numpy as np

D_MODEL = 1024
BATCH = 16
SEQ = 256
DEPTH = 1
DEC_BATCH = 2
DEC_SEQ = 4096
PAST_LEN = 512

GRID_W = 64
POS_BASE = 10000.0
D_SSD = 1024
SSD_HEAD_DIM = 64
SSD_HEADS = D_SSD // SSD_HEAD_DIM
SSD_GROUPS = 2
SSD_HPG = SSD_HEADS // SSD_GROUPS
SSD_STATE = 128
SSD_CONV = 3
SSD_CHUNK = 128
SSD_CONV_DIM = D_SSD + 2 * SSD_GROUPS * SSD_STATE
D_SGU = 1024
SGU_GROUPS = 8
SGU_GROUP_DIM = D_SGU // SGU_GROUPS
SGU_CHUNK = 128
D_FF = 4 * D_MODEL
D_IN_PROJ = D_SSD + SSD_CONV_DIM + 2 * SSD_HEADS + 2 * D_SGU + 2 * D_MODEL
DN_ALPHA = (2.0 * DEPTH) ** 0.25
DN_BETA = (8.0 * DEPTH) ** -0.25
EPS = 1e-5

kernel_name = 'hybrid_ssd_sgu_diffusion_step'


def layer_norm(x, g, b):
    xf = x.astype(jnp.float32)
    mu = jnp.mean(xf, axis=-1, keepdims=True)
    var = jnp.mean(jnp.square(xf - mu), axis=-1, keepdims=True)
    return ((xf - mu) * lax.rsqrt(var + EPS)).astype(x.dtype) * g + b


def rms_norm(x, g):
    xf = x.astype(jnp.float32)
    ms = jnp.mean(jnp.square(xf), axis=-1, keepdims=True)
    return (xf * lax.rsqrt(ms + EPS)).astype(x.dtype) * g


def grid_pos_embed(n_tokens, dtype):
    rows = n_tokens // GRID_W
    quarter = D_MODEL // 4
    omega = 1.0 / (POS_BASE ** (jnp.arange(quarter, dtype=jnp.float32) / quarter))
    r = jnp.arange(rows, dtype=jnp.float32)[:, None] * omega
    col = jnp.arange(GRID_W, dtype=jnp.float32)[:, None] * omega
    emb_r = jnp.concatenate([jnp.sin(r), jnp.cos(r)], axis=-1)
    emb_c = jnp.concatenate([jnp.sin(col), jnp.cos(col)], axis=-1)
    emb = jnp.concatenate([
        jnp.broadcast_to(emb_r[:, None, :], (rows, GRID_W, D_MODEL // 2)),
        jnp.broadcast_to(emb_c[None, :, :], (rows, GRID_W, D_MODEL // 2))], axis=-1)
    return emb.reshape(rows * GRID_W, D_MODEL).astype(dtype)


def depthwise_conv(x, w, b):
    out = lax.conv_general_dilated(
        x, w[:, None, :], window_strides=(1,), padding='SAME',
        dimension_numbers=('NWC', 'WIO', 'NWC'), feature_group_count=x.shape[-1])
    return out + b


def ssd_chunked_scan(x, dt, a_neg, B, C, h0):
    out_dtype = x.dtype
    f32 = jnp.float32
    bsz, L = x.shape[0], x.shape[1]
    nc = L // SSD_CHUNK
    xs = x.astype(f32).reshape(bsz, nc, SSD_CHUNK, SSD_GROUPS, SSD_HPG, SSD_HEAD_DIM)
    dtc = dt.astype(f32).reshape(bsz, nc, SSD_CHUNK, SSD_GROUPS, SSD_HPG)
    a = dtc * a_neg.astype(f32).reshape(SSD_GROUPS, SSD_HPG)
    xdt = xs * dtc[..., None]
    Bc = B.astype(f32).reshape(bsz, nc, SSD_CHUNK, SSD_GROUPS, SSD_STATE)
    Cc = C.astype(f32).reshape(bsz, nc, SSD_CHUNK, SSD_GROUPS, SSD_STATE)
    acum = jnp.moveaxis(jnp.cumsum(a, axis=2), 2, -1)
    seg = acum[..., :, None] - acum[..., None, :]
    lower = jnp.tril(jnp.ones((SSD_CHUNK, SSD_CHUNK), dtype=bool))
    decay = jnp.exp(jnp.where(lower, seg, -jnp.inf))
    cb = jnp.einsum('bcign,bcjgn->bcgij', Cc, Bc)
    y_diag = jnp.einsum('bcgij,bcgrij,bcjgrp->bcigrp', cb, decay, xdt)
    a_last = acum[..., -1]
    decay_to_end = jnp.exp(a_last[..., None] - acum)
    states = jnp.einsum('bcjgn,bcgrj,bcjgrp->bcgrpn', Bc, decay_to_end, xdt)
    h_init = h0.astype(f32).reshape(bsz, SSD_GROUPS, SSD_HPG, SSD_HEAD_DIM, SSD_STATE)

    def step(h, inp):
        st, al = inp
        return jnp.exp(al)[..., None, None] * h + st, h

    h_final, h_starts = lax.scan(step, h_init,
                                 (jnp.moveaxis(states, 1, 0), jnp.moveaxis(a_last, 1, 0)))
    h_starts = jnp.moveaxis(h_starts, 0, 1)
    y_off = jnp.einsum('bcign,bcgri,bcgrpn->bcigrp', Cc, jnp.exp(acum), h_starts)
    y = (y_diag + y_off).reshape(bsz, L, SSD_HEADS, SSD_HEAD_DIM)
    return (y.astype(out_dtype),
            h_final.reshape(bsz, SSD_HEADS, SSD_HEAD_DIM, SSD_STATE).astype(out_dtype))


def ssd_bidirectional(z, xbc, dt_raw, h0_f, h0_b, conv_w, conv_b, dt_bias_f, dt_bias_b,
                      a_log_f, a_log_b, d_skip, norm_w):
    bsz, L, _ = z.shape
    gn = SSD_GROUPS * SSD_STATE
    xbc = jax.nn.silu(depthwise_conv(xbc, conv_w, conv_b))
    xh = xbc[..., :D_SSD].reshape(bsz, L, SSD_HEADS, SSD_HEAD_DIM)
    B = xbc[..., D_SSD:D_SSD + gn].reshape(bsz, L, SSD_GROUPS, SSD_STATE)
    C = xbc[..., D_SSD + gn:].reshape(bsz, L, SSD_GROUPS, SSD_STATE)
    dt_f = jax.nn.softplus(dt_raw[..., :SSD_HEADS] + dt_bias_f)
    dt_b = jax.nn.softplus(dt_raw[..., SSD_HEADS:] + dt_bias_b)
    y_f, h_f = ssd_chunked_scan(xh, dt_f, -jnp.exp(a_log_f), B, C, h0_f)
    flip = lambda t: jnp.flip(t, axis=1)
    y_b, h_b = ssd_chunked_scan(flip(xh), flip(dt_b), -jnp.exp(a_log_b), flip(B), flip(C), h0_b)
    y = y_f + flip(y_b) + d_skip[:, None] * xh
    y = y.reshape(bsz, L, D_SSD)
    y = rms_norm(y * jax.nn.silu(z), norm_w)
    return y, h_f, h_b


def chunk_sgu(uv, ln_g, ln_b, w_spatial, b_spatial):
    uv = jax.nn.gelu(uv, approximate=False)
    u, v = uv[..., :D_SGU], uv[..., D_SGU:]
    v = layer_norm(v, ln_g, ln_b)
    bsz, L, _ = v.shape
    nc = L // SGU_CHUNK
    vg = v.reshape(bsz, nc, SGU_CHUNK, SGU_GROUPS, SGU_GROUP_DIM)
    s = jnp.einsum('gij,bcjgd->bcigd', w_spatial, vg) + jnp.swapaxes(b_spatial, 0, 1)[None, None, :, :, None]
    return u * s.reshape(bsz, L, D_SGU)


def trunk_layer(x, mod, h0_f, h0_b, w_in, conv_w, conv_b, dt_bias_f, dt_bias_b, a_log_f, a_log_b,
                d_skip, ssd_norm_w, sgu_ln_g, sgu_ln_b, w_spatial, b_spatial, w_proj_a, w_proj_b,
                w_out, ln1_g, ln1_b, w_ff1, w_ff2, ln2_g, ln2_b):
    shift1, scale1, gate1, shift2, scale2, gate2 = jnp.split(mod, 6, axis=-1)
    h = x * (1.0 + scale1) + shift1
    proj = h @ w_in
    o1 = D_SSD
    o2 = o1 + SSD_CONV_DIM
    o3 = o2 + 2 * SSD_HEADS
    o4 = o3 + 2 * D_SGU
    z, xbc, dt_raw, uv, g_logits = (proj[..., :o1], proj[..., o1:o2], proj[..., o2:o3],
                                    proj[..., o3:o4], proj[..., o4:])
    y_a, h_f, h_b = ssd_bidirectional(z, xbc, dt_raw, h0_f, h0_b, conv_w, conv_b, dt_bias_f,
                                      dt_bias_b, a_log_f, a_log_b, d_skip, ssd_norm_w)
    y_b = chunk_sgu(uv, sgu_ln_g, sgu_ln_b, w_spatial, b_spatial)
    gates = jax.nn.sigmoid(g_logits)
    merged = gates[..., :D_MODEL] * (y_a @ w_proj_a) + gates[..., D_MODEL:] * (y_b @ w_proj_b)
    x = layer_norm(DN_ALPHA * x + gate1 * (merged @ w_out), ln1_g, ln1_b)
    h = x * (1.0 + scale2) + shift2
    f = jnp.square(jax.nn.relu(h @ w_ff1)) @ w_ff2
    x = layer_norm(DN_ALPHA * x + gate2 * f, ln2_g, ln2_b)
    return x, h_f, h_b


def setup_inputs(seed: int = 0) -> dict:
    key = jax.random.key(seed)
    ks = jax.random.split(key, 40)
    f32 = jnp.float32
    nrm = lambda k, shape, scale: jax.random.normal(k, shape, f32) * scale
    L = DEPTH
    dt0 = jnp.exp(jax.random.uniform(ks[10], (L, SSD_HEADS), f32, math.log(1e-3), math.log(1e-1)))
    dt1 = jnp.exp(jax.random.uniform(ks[11], (L, SSD_HEADS), f32, math.log(1e-3), math.log(1e-1)))
    inv_softplus = lambda d: d + jnp.log(-jnp.expm1(-d))
    return {
        'x_prompt': nrm(ks[0], (BATCH, SEQ, D_MODEL), 1.0),
        'x_sample': nrm(ks[1], (DEC_BATCH, DEC_SEQ, D_MODEL), 1.0),
        'state_ssd_fwd': nrm(ks[2], (DEC_BATCH, DEPTH, SSD_HEADS, SSD_HEAD_DIM, SSD_STATE), 0.5),
        'state_ssd_bwd': nrm(ks[3], (DEC_BATCH, DEPTH, SSD_HEADS, SSD_HEAD_DIM, SSD_STATE), 0.5),
        'c': nrm(ks[4], (DEC_BATCH, D_MODEL), 1.0),
        'c_ctx': nrm(ks[5], (D_MODEL,), 1.0),
        'w_ada': nrm(ks[6], (L, D_MODEL, 6 * D_MODEL), 0.5 * D_MODEL ** -0.5),
        'b_ada': nrm(ks[7], (L, 6 * D_MODEL), 0.1),
        'w_in': nrm(ks[8], (L, D_MODEL, D_IN_PROJ), D_MODEL ** -0.5),
        'conv_w': nrm(ks[9], (L, SSD_CONV, SSD_CONV_DIM), SSD_CONV ** -0.5),
        'conv_b': nrm(ks[12], (L, SSD_CONV_DIM), 0.02),
        'dt_bias_fwd': inv_softplus(dt0),
        'dt_bias_bwd': inv_softplus(dt1),
        'a_log_fwd': jnp.log(jax.random.uniform(ks[13], (L, SSD_HEADS), f32, 1.0, 16.0)),
        'a_log_bwd': jnp.log(jax.random.uniform(ks[14], (L, SSD_HEADS), f32, 1.0, 16.0)),
        'd_skip': 1.0 + nrm(ks[15], (L, SSD_HEADS), 0.1),
        'ssd_norm_w': 1.0 + nrm(ks[16], (L, D_SSD), 0.05),
        'sgu_ln_g': 1.0 + nrm(ks[17], (L, D_SGU), 0.05),
        'sgu_ln_b': nrm(ks[18], (L, D_SGU), 0.02),
        'w_spatial': nrm(ks[19], (L, SGU_GROUPS, SGU_CHUNK, SGU_CHUNK), SGU_CHUNK ** -0.5),
        'b_spatial': 1.0 + nrm(ks[20], (L, SGU_GROUPS, SGU_CHUNK), 0.05),
        'w_proj_a': nrm(ks[21], (L, D_SSD, D_MODEL), D_SSD ** -0.5),
        'w_proj_b': nrm(ks[22], (L, D_SGU, D_MODEL), D_SGU ** -0.5),
        'w_out': nrm(ks[23], (L, D_MODEL, D_MODEL), DN_BETA * D_MODEL ** -0.5),
        'ln1_g': 1.0 + nrm(ks[24], (L, D_MODEL), 0.05),
        'ln1_b': nrm(ks[25], (L, D_MODEL), 0.02),
        'w_ff1': nrm(ks[26], (L, D_MODEL, D_FF), D_MODEL ** -0.5),
        'w_ff2': nrm(ks[27], (L, D_FF, D_MODEL), DN_BETA * D_FF ** -0.5),
        'ln2_g': 1.0 + nrm(ks[28], (L, D_MODEL), 0.05),
        'ln2_b': nrm(ks[29], (L, D_MODEL), 0.02),
    }


def reference(x_prompt, x_sample, state_ssd_fwd, state_ssd_bwd, c, c_ctx, w_ada, b_ada, w_in,
              conv_w, conv_b, dt_bias_fwd, dt_bias_bwd, a_log_fwd, a_log_bwd, d_skip, ssd_norm_w,
              sgu_ln_g, sgu_ln_b, w_spatial, b_spatial, w_proj_a, w_proj_b, w_out, ln1_g, ln1_b,
              w_ff1, w_ff2, ln2_g, ln2_b):
    xp = x_prompt
    xs = x_sample + grid_pos_embed(x_sample.shape[1], x_sample.dtype)[None]
    silu_ctx = jax.nn.silu(c_ctx)
    silu_c = jax.nn.silu(c)
    new_f, new_b = [], []
    for l in range(DEPTH):
        p = (w_in[l], conv_w[l], conv_b[l], dt_bias_fwd[l], dt_bias_bwd[l], a_log_fwd[l],
             a_log_bwd[l], d_skip[l], ssd_norm_w[l], sgu_ln_g[l], sgu_ln_b[l], w_spatial[l],
             b_spatial[l], w_proj_a[l], w_proj_b[l], w_out[l], ln1_g[l], ln1_b[l], w_ff1[l],
             w_ff2[l], ln2_g[l], ln2_b[l])
        mod_ctx = (silu_ctx @ w_ada[l] + b_ada[l])[None, None, :]
        h0 = jnp.zeros((xp.shape[0], SSD_HEADS, SSD_HEAD_DIM, SSD_STATE), xp.dtype)
        xp, h_f, h_b = trunk_layer(xp, mod_ctx, h0, h0, *p)
        new_f.append(h_f)
        new_b.append(h_b)
        mod_lat = (silu_c @ w_ada[l] + b_ada[l])[:, None, :]
        xs, _, _ = trunk_layer(xs, mod_lat, state_ssd_fwd[:, l], state_ssd_bwd[:, l], *p)
    new_state_ssd_fwd = jnp.stack(new_f, axis=1)
    new_state_ssd_bwd = jnp.stack(new_b, axis=1)
    return (xp, xs, new_state_ssd_fwd, new_state_ssd_bwd)
```

```python
import math
import numpy as np
from contextlib import ExitStack
import concourse.bass as bass
import concourse.mybir as mybir
from concourse.bass_utils import run_bass_kernel_spmd

F32 = mybir.dt.float32
BF16 = mybir.dt.bfloat16
AF = mybir.ActivationFunctionType
ALU = mybir.AluOpType
I32 = mybir.dt.int32
DT_SIZE = {F32: 4, BF16: 2, I32: 4}

D = 1024
NCH = 4
EXT = 130
NT = 10
DFF = 4096
DIN = 6688
EPS = 1e-5
ALPHA = 2.0 ** 0.25
NRING = 4


class Op:
    __slots__ = ("eng", "fn", "deps", "idx", "needs_sig", "sigval", "dma", "semkey", "semval", "chain",
                 "size", "pos", "inexact", "sync_same", "cost", "nbytes", "psrc", "stage", "order_only", "actset")

    def __init__(self, eng, fn):
        self.eng = eng
        self.fn = fn
        self.deps = set()
        self.needs_sig = False
        self.sigval = 0
        self.dma = False
        self.semkey = None
        self.semval = 0
        self.chain = True
        self.size = 64
        self.pos = 0
        self.inexact = set()
        self.sync_same = set()
        self.order_only = set()
        self.actset = None
        self.cost = None
        self.nbytes = 0
        self.psrc = False


class Sched:
    ENGS = ["sp", "act", "dve", "pool", "pe"]
    BLK = {"sp": "sync", "act": "scalar", "dve": "vector", "pool": "gpsimd", "pe": "tensor"}

    def __init__(self, nc):
        self.nc = nc
        self.ops = []
        self.regions = {}
        self.semkeys = {}
        self.engpos = {}
        self.stage = "init"
        self.prio_mode = "blevel"

    def _region(self, ap):
        t = ap.tensor
        if type(t).__name__.startswith("DRam"):
            return None
        if type(t).__name__.startswith("PSum"):
            return (t.name, 0, 2048, True)
        esz = DT_SIZE[ap.dtype]
        pstride = 1
        for d in list(t.shape)[1:]:
            pstride *= int(d)
        lo = int(ap.offset) % pstride
        hi = lo + 1
        for step, cnt in list(ap.ap)[1:]:
            hi += (int(cnt) - 1) * abs(int(step))
        return (t.name, lo * esz, hi * esz, False)

    def _access(self, reg, op, is_write):
        name, lo, hi, psum = reg
        if psum:
            is_write = True
        lst = self.regions.get(name, [])
        new = []
        for rec in lst:
            rlo, rhi, rop, rw = rec
            if rlo < hi and lo < rhi:
                if (is_write or rw) and rop is not op:
                    op.deps.add(rop)
                    if rw and not (rlo == lo and rhi == hi):
                        op.inexact.add(rop)
                if is_write and lo <= rlo and rhi <= hi:
                    continue
                if (not is_write) and (not rw) and rop.eng == op.eng and lo <= rlo and rhi <= hi and not rop.dma:
                    if rop is not op:
                        op.deps.add(rop)
                        op.order_only.add(rop)
                    continue
            new.append(rec)
        new.append((lo, hi, op, is_write))
        self.regions[name] = new

    def add(self, eng, fn, reads=(), writes=(), dma=False, semkey=None, chain=True, cost=None, actset=None):
        op = Op(eng, fn)
        op.idx = len(self.ops)
        op.cost = cost
        op.actset = actset
        op.stage = self.stage
        sz = 64
        for ap in list(reads) + list(writes):
            n = 1
            for step, cnt in list(ap.ap)[1:]:
                n *= int(cnt)
            sz = max(sz, n)
            if type(ap.tensor).__name__.startswith("PSum"):
                op.psrc = True
        op.size = sz
        if dma:
            ap = writes[0]
            n = int(list(ap.ap)[0][1])
            for step, cnt in list(ap.ap)[1:]:
                n *= int(cnt)
            op.nbytes = n * 4
        op.pos = self.engpos.get(eng, 0)
        self.engpos[eng] = op.pos + sz
        for ap in reads:
            r = self._region(ap)
            if r:
                self._access(r, op, False)
        for ap in writes:
            r = self._region(ap)
            if r:
                self._access(r, op, True)
        if dma:
            op.dma = True
            op.semkey = semkey
            op.chain = chain
            k = self.semkeys.setdefault(semkey, {"count": 0, "last": None})
            if chain and k["last"] is not None:
                op.deps.add(k["last"])
            k["count"] += 16
            k["last"] = op
            op.semval = k["count"]
        self.ops.append(op)
        return op

    def dma(self, eng, out, in_, semkey, chain=True):
        return self.add(eng, lambda e: e.dma_start(out=out, in_=in_), reads=[in_], writes=[out],
                        dma=True, semkey=semkey, chain=chain)

    def _opcost(self, op):
        if op.cost is not None:
            return op.cost
        if op.dma:
            return 1.0 if op.eng == "pool" else 0.15
        if op.eng == "dve":
            return op.size / 960.0 + (0.2 if op.psrc else 0.157)
        if op.eng == "act":
            return op.size / 1400.0 + 0.22
        return op.size * 2.2 / 1000.0 + 0.15

    def list_schedule(self):
        import heapq
        ops = self.ops
        n = len(ops)
        succ = {}
        indeg = {}
        for op in ops:
            indeg[op] = len(op.deps)
            for d in op.deps:
                succ.setdefault(d, []).append(op)
        blevel = {}
        if self.prio_mode == "blevel":
            for op in reversed(ops):
                c = self._opcost(op) + ((op.nbytes / 280e3 + 2.0) if op.dma else 0.0)
                b = 0.0
                for sop in succ.get(op, ()):
                    if blevel[sop] > b:
                        b = blevel[sop]
                blevel[op] = b + c
        fin = {}
        self.sim_start = {}
        eng_free = {e: 0.0 for e in self.ENGS}
        dma_free = [0.0]
        ready = {e: [] for e in self.ENGS}
        for op in ops:
            if indeg[op] == 0:
                heapq.heappush(ready[op.eng], (0.0, op.idx, op))
        order = []
        cur_set = [None]
        LAT = 0.4
        while len(order) < n:
            best = None
            for e in self.ENGS:
                h = ready[e]
                if not h:
                    continue
                rt, idx, op = h[0]
                st = max(rt, eng_free[e])
                if best is None or (st, idx) < (best[0], best[1]):
                    best = (st, idx, e)
            st, idx, e = best
            h = ready[e]
            cand = []
            while h and h[0][0] <= st + 1e-9:
                cand.append(heapq.heappop(h))
            if self.prio_mode == "blevel":
                if e == "act":
                    cand.sort(key=lambda t: (t[2].actset is not None and t[2].actset != cur_set[0], -blevel[t[2]], t[1]))
                else:
                    cand.sort(key=lambda t: (-blevel[t[2]], t[1]))
            else:
                cand.sort(key=lambda t: t[1])
            rt, idx, op = cand[0]
            for c in cand[1:]:
                heapq.heappush(h, c)
            c = self._opcost(op)
            if e == "act" and op.actset is not None and op.actset != cur_set[0]:
                c += 1.3
                cur_set[0] = op.actset
            eng_free[e] = st + c
            if op.dma:
                t0 = max(st + c, dma_free[0])
                dur = op.nbytes / 280e3
                dma_free[0] = t0 + dur
                fin[op] = t0 + dur + 2.0
            else:
                fin[op] = st + c
            order.append(op)
            self.sim_start[op] = st
            for sop in succ.get(op, ()):
                indeg[sop] -= 1
                if indeg[sop] == 0:
                    rt2 = 0.0
                    for d in sop.deps:
                        lat = 0.0 if (d.eng == sop.eng and not d.dma) else LAT
                        rt2 = max(rt2, fin[d] + lat)
                    heapq.heappush(ready[sop.eng], (rt2, sop.idx, sop))
        self.ops = order
        self.model_time = max(fin.values())
        pos = {e: 0 for e in self.ENGS}
        for i, op in enumerate(order):
            op.idx = i
            op.pos = pos[op.eng]
            pos[op.eng] += op.size

    def emit(self, final_wait_keys=(), reorder=True):
        nc = self.nc
        if reorder:
            self.list_schedule()
        for op in self.ops:
            for d in op.deps:
                if d.eng != op.eng:
                    d.needs_sig = True
                elif op.eng in ("act", "dve", "pool") and not d.dma and d not in op.order_only:
                    gap = op.pos - (d.pos + d.size)
                    if gap >= 512:
                        continue
                    if d.size >= 512 and d not in op.inexact:
                        continue
                    d.needs_sig = True
                    op.sync_same.add(d)
        counters = {e: 0 for e in self.ENGS}
        for op in self.ops:
            if not op.dma and op.needs_sig:
                counters[op.eng] += 1
                op.sigval = counters[op.eng]
        byeng = {e: [] for e in self.ENGS}
        for op in self.ops:
            byeng[op.eng].append(op)
        with ExitStack() as es:
            engsem = {e: es.enter_context(nc.semaphore("sem_" + e)) for e in self.ENGS}
            dmasem = {k: es.enter_context(nc.semaphore("dsem_%d" % i)) for i, k in enumerate(self.semkeys)}
            block = es.enter_context(nc.Block())
            for eng in self.ENGS:
                def body(e, eng=eng):
                    waited = {}
                    for op in byeng[eng]:
                        need = {}
                        for d in op.deps:
                            if d.dma:
                                sem = dmasem[d.semkey]
                                val = d.semval if d.chain else self.semkeys[d.semkey]["count"]
                                key = "d_" + str(d.semkey)
                            else:
                                if d.eng == eng and d not in op.sync_same:
                                    continue
                                sem = engsem[d.eng]
                                val = d.sigval
                                key = "e_" + d.eng
                            if waited.get(key, 0) >= val:
                                continue
                            if need.get(key, (None, 0))[1] < val:
                                need[key] = (sem, val)
                        for key, (sem, val) in need.items():
                            e.wait_ge(sem, val)
                            waited[key] = val
                        ins = op.fn(e)
                        if op.dma:
                            ins.then_inc(dmasem[op.semkey], 16)
                        elif op.needs_sig:
                            ins.then_inc(engsem[eng], 1)
                    if eng == "sp":
                        for k in final_wait_keys:
                            e.wait_ge(dmasem[k], self.semkeys[k]["count"])
                getattr(block, self.BLK[eng])(body)


class Arena:
    def __init__(self, nc, es, name, nbytes):
        self.nbytes = nbytes
        self.t32 = es.enter_context(nc.sbuf_tensor(name, [128, nbytes // 4], F32))
        self.t16 = self.t32.bitcast(BF16)
        assert self.t16.name == self.t32.name
        self.off = 0

    def view(self, off_bytes, shape_free, dtype):
        n = 1
        for d in shape_free:
            n *= d
        esz = DT_SIZE[dtype]
        assert off_bytes % 4 == 0
        assert off_bytes + n * esz <= self.nbytes, ("arena overflow", off_bytes, n * esz, self.nbytes)
        base = self.t32 if dtype == F32 else self.t16
        o = off_bytes // esz
        ap = base[:, o:o + n]
        if len(shape_free) == 1:
            return ap
        names = " ".join("d%d" % i for i in range(len(shape_free)))
        kw = {"d%d" % i: shape_free[i] for i in range(len(shape_free))}
        return ap.rearrange("p (%s) -> p %s" % (names, names), **kw)

    def alloc(self, shape_free, dtype):
        n = 1
        for d in shape_free:
            n *= d
        nb = (n * DT_SIZE[dtype] + 3) // 4 * 4
        v = self.view(self.off, shape_free, dtype)
        self.off += nb
        return v


def fpat(ap, pattern_free, extra_off=0):
    p = list(ap.ap)[0]
    return bass.AP(ap.tensor, int(ap.offset) + extra_off,
                   [[int(p[0]), int(p[1])]] + [[int(a), int(b)] for a, b in pattern_free])


class _Stop(Exception):
    pass


OFFLOAD = {"xdP", "HcP"}


def build_program(upto=None, dumps=None):
    nc = bass.Bass("TRN2", target_bir_lowering=False)
    dbg_list = []

    def chk(name):
        S.stage = name + "+"
        if upto == name:
            raise _Stop()

    def din(name, shape):
        return nc.dram_tensor(name, list(shape), F32, kind="ExternalInput").ap()

    def dout(name, shape):
        return nc.dram_tensor(name, list(shape), F32, kind="ExternalOutput").ap()

    xres = din("xres", [3, 512, D])
    xTe = din("xTe", [NT, D, NCH * EXT])
    d_selT = din("selT", [NT - 1, 64, NCH * EXT])
    d_selc = din("selc", [64, EXT])
    h0f = din("h0f", [D, 128])
    h0b = din("h0b", [D, 128])
    w_ada = din("w_ada", [D, 6 * D])
    w_in = din("w_in", [D, DIN])
    w_pa = din("w_pa", [D, D])
    w_pb = din("w_pb", [D, D])
    w_out = din("w_out", [D, D])
    w_ff1 = din("w_ff1", [D, DFF])
    w_ff2 = din("w_ff2", [DFF, D])
    d_badacol = din("b_ada_col", [128, 48])
    d_badarow = din("b_ada_row", [1, 6 * D])
    d_ccol = din("ccol", [128, 16])
    d_convw = din("convw", [128, 36])
    d_convb = din("convb", [128, 12])
    d_dtb = din("dtb", [1, 32])
    d_alog = din("alog", [1, 32])
    d_dskip = din("dskip", [1, 16])
    d_normw = din("normw_col", [128, 8])
    d_sgug = din("sgu_g", [1, D])
    d_sgub = din("sgu_b", [1, D])
    d_ln1g = din("ln1_g", [1, D])
    d_ln1b = din("ln1_b", [1, D])
    d_ln2g = din("ln2_g", [1, D])
    d_ln2b = din("ln2_b", [1, D])
    d_wspT = din("wspT", [128, 1024])
    d_bsp = din("bsp", [1, 1024])
    d_ident = din("ident", [128, 128])
    d_mle = din("m_le", [128, 128])
    d_mge = din("m_ge", [128, 128])
    d_mgt = din("m_gt", [128, 128])
    d_mlt = din("m_lt", [128, 128])
    d_jidx = din("jidx", [64, 256])
    d_pidx = din("pidx", [64, 1])
    d_selcol = din("selcol", [64, 128])
    d_selcolh = din("selcol_h", [64, 8])
    d_selrow = din("selrow", [NT - 1, 64, 512])
    d_selrowh = din("selrow_h", [NT - 1, 64, 8])
    d_flags = din("flags", [128, NT * 16])
    d_swf = din("swf", [128, 4])

    o_yp = dout("yp", [512, D])
    o_ys = dout("ys", [1024, D])
    o_nsf = dout("nsf", [2, D, 128])
    o_nsb = dout("nsb", [2, D, 128])

    es = ExitStack()
    S = Sched(nc)
    A = Arena(nc, es, "arena", 204 * 1024)
    PSB = [es.enter_context(nc.psum_tensor("ps%d" % i, [128, 512], F32)) for i in range(8)]
    ps_ctr = [0]

    def PS():
        b = PSB[ps_ctr[0] % 8]
        ps_ctr[0] += 1
        return b

    def act(out, in_, func, bias=None, scale=None):
        kw = {}
        rd = [in_]
        if bias is not None:
            kw["bias"] = bias
            if not isinstance(bias, (int, float)):
                rd.append(bias)
        if scale is not None:
            kw["scale"] = scale
            if not isinstance(scale, (int, float)):
                rd.append(scale)
        aset = {AF.Exp: "exp", AF.Ln: "exp", AF.Identity: None, AF.Copy: None}.get(func, str(func))
        S.add("act", lambda e: e.activation(out=out, in_=in_, func=func, **kw), rd, [out], actset=aset)

    def tt(out, a, b, op, eng="dve"):
        cost = None
        if eng == "dve" and a.dtype == BF16 and b.dtype == BF16 and out.dtype == BF16:
            n = 1
            for step, cnt in list(out.ap)[1:]:
                n *= int(cnt)
            cost = n / 1920.0 + 0.157
        S.add(eng, lambda e: e.tensor_tensor(out=out, in0=a, in1=b, op=op), [a, b], [out], cost=cost)

    def ts(out, a, s1, op0, s2=None, op1=None, eng="dve"):
        rd = [a]
        if not isinstance(s1, (int, float)):
            rd.append(s1)
        if s2 is not None and not isinstance(s2, (int, float)):
            rd.append(s2)
        if op1 is None:
            S.add(eng, lambda e: e.tensor_scalar(out=out, in0=a, scalar1=s1, scalar2=None, op0=op0), rd, [out])
        else:
            S.add(eng, lambda e: e.tensor_scalar(out=out, in0=a, scalar1=s1, scalar2=s2, op0=op0, op1=op1), rd, [out])

    def stt(out, a, s, b, op0, op1):
        rd = [a, b]
        if not isinstance(s, (int, float)):
            rd.append(s)
        S.add("dve", lambda e: e.scalar_tensor_tensor(out=out, in0=a, scalar=s, in1=b, op0=op0, op1=op1), rd, [out])

    PEN = "pool"
    OFF = OFFLOAD

    def cp(out, in_, eng="dve"):
        if eng == "act":
            act(out, in_, AF.Identity)
        else:
            S.add(eng, lambda e: e.tensor_copy(out=out, in_=in_), [in_], [out])

    def memset(ap, v, eng="dve"):
        S.add(eng, lambda e: e.memset(ap, v), [], [ap])

    def mm(out, lhsT, rhs, start=True, stop=True):
        n = 1
        for step, cnt in list(rhs.ap)[1:]:
            n *= int(cnt)
        c = max(max(n, 16) / 1950.0 * (4.0 if lhsT.dtype == F32 else 1.0), 0.035) + (0.08 if lhsT.dtype == F32 else 0.0)
        S.add("pe", lambda e: e.matmul(out, lhsT=lhsT, rhs=rhs, start=start, stop=stop), [lhsT, rhs], [out], cost=c)

    def tr(out, in_, identity):
        c = 0.2 if in_.dtype == F32 else 0.1
        S.add("pe", lambda e: e.transpose(out=out, in_=in_, identity=identity), [in_, identity], [out], cost=c)

    ident = A.alloc([128], F32)
    identb = A.alloc([128], BF16)
    m_le = A.alloc([128], F32)
    m_ge = A.alloc([128], F32)
    m_gt = A.alloc([128], F32)
    m_lt = A.alloc([128], F32)
    m_gtb = A.alloc([128], BF16)
    m_leb = A.alloc([128], BF16)
    m_geb = A.alloc([128], BF16)
    m_ltb = A.alloc([128], BF16)
    onesf = A.alloc([128], F32)
    onesb = A.alloc([128], BF16)
    cst = A.alloc([8], F32)
    lnA = A.alloc([D], F32)
    lnB = A.alloc([D], F32)
    gates = A.alloc([2, 2, D], F32)
    modcol = A.alloc([4, 8, 2], F32)
    convw = A.alloc([12, 3], F32)
    convb = A.alloc([12], F32)
    dtb = A.alloc([32], F32)
    aneg = A.alloc([32], F32)
    dskip = A.alloc([16], F32)
    normw = A.alloc([8], F32)
    flags = A.alloc([NT * 16], F32)
    swf = A.alloc([4], F32)
    Epos = A.alloc([512], F32)
    Eposb = A.alloc([512], BF16)
    posc = A.alloc([512], F32)
    posch = A.alloc([512], F32)
    selcol = A.alloc([128], F32)
    selcolh = A.alloc([8], F32)
    selc = A.alloc([EXT], F32)
    PCT = A.alloc([4, EXT], F32)
    jidx = A.alloc([256], F32)
    pidx = A.alloc([2], F32)
    wspT = A.alloc([8, 128], BF16)
    bspf = A.alloc([1024], F32)
    bsph = A.alloc([1024], BF16)
    bspl = A.alloc([1024], BF16)
    Hf0 = A.alloc([D], F32)
    Hf = A.alloc([D], F32)
    Hb = A.alloc([D], F32)
    ring = [A.alloc([8, 512], BF16) for _ in range(NRING)]
    badacol = A.alloc([48], F32)
    ccol = A.alloc([8, 2], F32)
    sc2 = A.alloc([8, 2], BF16)
    scr = A.alloc([64], F32)
    TB = A.off
    hT = A.view(TB, [8, NCH, EXT], BF16)
    R = TB + 8 * NCH * EXT * 2
    KB = 1024

    def RV(off, shape, dt):
        return A.view(R + off, shape, dt)

    def rowb(src):
        n = int(src.shape[-1])
        return bass.AP(src.tensor, int(src.offset), [[0, 128], [1, n]])

    def cl(dst, src):
        S.dma("sp", dst, src, "const", chain=False)

    cl(ident, d_ident); cl(m_le, d_mle); cl(m_ge, d_mge); cl(m_gt, d_mgt); cl(m_lt, d_mlt)
    cl(convw.rearrange("p a b -> p (a b)"), d_convw); cl(convb, d_convb)
    cl(dtb, rowb(d_dtb)); cl(aneg, rowb(d_alog))
    cl(dskip, rowb(d_dskip)); cl(normw, d_normw)
    cl(flags, d_flags)
    cl(swf, d_swf)
    cl(selcol[0:64, :], d_selcol); cl(selcolh[0:64, :], d_selcolh); cl(selc[0:64, :], d_selc)
    cl(jidx[0:64, :], d_jidx); cl(pidx[0:64, 0:1], d_pidx)
    cl(bspf[0:1, :], d_bsp)
    cl(badacol, d_badacol); cl(ccol.rearrange("p a b -> p (a b)"), d_ccol)
    S.dma("pool", wspT.rearrange("p a b -> p (a b)"), d_wspT, "constp", chain=False)

    memset(cst[:, 0:1], -math.pi)
    memset(cst[:, 1:2], 1.0)
    memset(cst[:, 2:3], EPS)
    memset(onesf, 1.0)
    memset(onesb, 1.0)
    cp(identb, ident)
    cp(m_gtb, m_gt)
    cp(m_leb, m_le)
    cp(m_geb, m_ge)
    cp(m_ltb, m_lt)
    act(aneg, aneg, AF.Exp)
    ts(aneg, aneg, -1.0, ALU.mult)
    cp(bsph[0:1, :], bspf[0:1, :])
    tt(bspf[0:1, :], bspf[0:1, :], bsph[0:1, :], ALU.subtract)
    cp(bspl[0:1, :], bspf[0:1, :])

    ring_ctr = [0]

    def wtile(src, ncols):
        i = ring_ctr[0] % NRING
        ring_ctr[0] += 1
        dst = ring[i][:, :, 0:ncols]
        S.dma("pool", dst, src.rearrange("(k p) c -> p k c", p=128), "w%d" % i)
        return dst

    def build_pos():
        E = Epos[0:64, :]
        om = RV(0, [256], F32)[0:64, :]
        ang = RV(1 * KB, [256], F32)[0:64, :]
        r = RV(2 * KB, [256], F32)[0:64, :]
        kf = RV(3 * KB, [256], F32)[0:64, :]
        ki = RV(4 * KB, [256], F32)[0:64, :].bitcast(I32)
        m = RV(5 * KB, [256], F32)[0:64, :]
        y = RV(6 * KB, [256], F32)[0:64, :]
        act(om, jidx[0:64, :], AF.Exp, scale=-math.log(10000.0) / 256.0)
        ts(ang, om, pidx[0:64, 0:1], ALU.mult)
        ts(r, ang, 1.0 / (2 * math.pi), ALU.mult)
        cp(ki, r)
        cp(kf, ki)
        stt(y, kf, -2 * math.pi, ang, ALU.mult, ALU.add)
        for half, shift in ((0, 0.0), (1, math.pi / 2)):
            ts(r, y, shift, ALU.add)
            ts(m, r, math.pi, ALU.is_gt)
            stt(r, m, -2 * math.pi, r, ALU.mult, ALU.add)
            ts(r, r, 3.1415925, ALU.min, -3.1415925, ALU.max)
            act(E[:, half * 256:(half + 1) * 256], r, AF.Sin)
        p = PS()
        mm(p[:, :], selcol[0:64, :], E)
        cp(posc, p[:, :])
        p = PS()
        mm(p[0:8, :], selcolh[0:64, :], E)
        cp(posch[0:8, :], p[0:8, :])
        cp(Eposb[0:64, :], E)
        for h2 in range(2):
            p = PS()
            for q in range(2):
                fc = h2 * 2 + q
                mm(p[:, q * EXT:(q + 1) * EXT], E[:, fc * 128:(fc + 1) * 128], selc[0:64, :])
            cp(PCT[:, h2 * 2:h2 * 2 + 2, :].rearrange("p a b -> p (a b)"), p[:, 0:2 * EXT])

    def mod_init():
        act(ccol, ccol, AF.Silu)
        cp(sc2, ccol)

    def mod_tiles(tiles):
        kindmap = {0: 0, 1: 1, 3: 2, 4: 3}
        for t in tiles:
            c0 = t * 512
            kind = c0 // D
            half = (c0 % D) // 512
            wt = wtile(w_ada[:, c0:c0 + 512], 512)
            if kind in (2, 5):
                wi = 0 if kind == 2 else 1
                bb = lnA[:, half * 512:(half + 1) * 512]
                S.dma("sp", bb, rowb(d_badarow[:, c0:c0 + 512]), "misc")
                for cond in range(2):
                    p = PS()
                    for k in range(8):
                        mm(p[:, :], fpat(sc2[:, k, cond:cond + 1], [[0, 128]]), wt[:, k, :], k == 0, k == 7)
                    tt(gates[:, cond, wi, half * 512:(half + 1) * 512], p[:, :], bb, ALU.add)
            else:
                km = kindmap[kind]
                for s in range(4):
                    fc = half * 4 + s
                    p = PS()
                    for k in range(8):
                        mm(p[:, 0:2], wt[:, k, s * 128:(s + 1) * 128], sc2[:, k, :], k == 0, k == 7)
                    col = kind * 8 + fc
                    ts(modcol[:, km, fc, :], p[:, 0:2], badacol[:, col:col + 1], ALU.add,
                       1.0 if km in (1, 3) else 0.0, ALU.add)

    def load_x(ti, x_tm, xh_rows, with_halo):
        assert not with_halo
        S.dma("sp", x_tm, xres[{0: 0, 8: 1, 9: 2}[ti], :, :].rearrange("(c p) f -> p c f", p=128), "x")
        if ti >= 1:
            pi = ti - 1
            selrow = RV(84 * KB, [512], F32)[0:64, :]
            selrowh = RV(86 * KB, [8], F32)[0:64, :]
            S.dma("sp", selrow, d_selrow[pi], "misc")
            for c in range(NCH):
                p = PS()
                mm(p[:, :], selrow[:, c * 128:(c + 1) * 128], Epos[0:64, :])
                tt(x_tm[:, c, 0:512], x_tm[:, c, 0:512], p[:, :], ALU.add)
                tt(x_tm[:, c, 512:1024], x_tm[:, c, 512:1024], posc, ALU.add)
            if with_halo:
                S.dma("sp", selrowh, d_selrowh[pi], "misc")
                p = PS()
                mm(p[0:8, :], selrowh, Epos[0:64, :])
                tt(xh_rows[:, 0:512], xh_rows[:, 0:512], p[0:8, :], ALU.add)
                tt(xh_rows[:, 512:1024], xh_rows[:, 512:1024], posch[0:8, :], ALU.add)

    def make_hT(x_tm, xh_rows, cond, kshift, kscale, ti, with_halo):
        for f in range(8):
            p = PS()
            for c in range(NCH):
                tr(p[:, c * 128:(c + 1) * 128], x_tm[:, c, f * 128:(f + 1) * 128], ident)
            act(hT[:, f, :, 1:129], p[:, :].rearrange("p (c t) -> p c t", c=NCH), AF.Identity,
                bias=modcol[:, kshift, f, cond:cond + 1], scale=modcol[:, kscale, f, cond:cond + 1])
        if with_halo:
            p = PS()
            for f in range(8):
                tr(p[:, f * 8:(f + 1) * 8], xh_rows[:, f * 128:(f + 1) * 128], ident[0:8, 0:8])
            for f in range(8):
                outv = fpat(hT[:, f, 0, 0:1], [[EXT, NCH], [EXT - 1, 2]])
                inv = p[:, f * 8:(f + 1) * 8].rearrange("p (c s) -> p c s", c=NCH)
                act(outv, inv, AF.Identity, bias=modcol[:, kshift, f, cond:cond + 1],
                    scale=modcol[:, kscale, f, cond:cond + 1])
            hv = fpat(hT[:, 0, 0, 0:1], [[NCH * EXT, 8], [EXT, NCH], [EXT - 1, 2]])
            fv = fpat(flags[:, ti * 16:ti * 16 + 1], [[0, 8], [4, NCH], [1, 2]])
            tt(hv, hv, fv, ALU.mult)

    HTC = [hT]

    def make_hT_fm(ti, cond, xT_st, selT_t):
        hTc = HTC[0]
        S.dma("sp", xT_st, xTe[ti].rearrange("(k p) t -> p k t", p=128), "x")
        if ti >= 1:
            S.dma("pool", selT_t, d_selT[ti - 1], "selt")
            for fc in range(4):
                for half in range(2):
                    p = PS()
                    mm(p[:, 0:2 * EXT], Eposb[0:64, fc * 128:(fc + 1) * 128], selT_t[:, half * 2 * EXT:(half + 1) * 2 * EXT])
                    xs = xT_st[:, fc, half * 2 * EXT:(half + 1) * 2 * EXT]
                    tt(xs, xs, p[:, 0:2 * EXT], ALU.add)
            xc = xT_st[:, 4:8, :].rearrange("p f (c e) -> p f c e", c=NCH)
            tt(xc, xc, fpat(PCT[:, 0, 0:1], [[EXT, 4], [0, NCH], [1, EXT]]), ALU.add, eng=PEN if "posP" in OFF else "dve")
        for f in range(8):
            act(hTc[:, f, :, :].rearrange("p c e -> p (c e)"), xT_st[:, f, :], AF.Identity,
                bias=modcol[:, 0, f, cond:cond + 1], scale=modcol[:, 1, f, cond:cond + 1])
        hv = fpat(hTc[:, 0, 0, 0:1], [[NCH * EXT, 8], [EXT, NCH], [EXT - 1, 2]])
        fv = fpat(flags[:, ti * 16:ti * 16 + 1], [[0, 8], [4, NCH], [1, 2]])
        tt(hv, hv, fv, ALU.mult)

    hT_main = lambda k: HTC[0][:, k, :, 1:129]
    hT_chunk = lambda k, c: HTC[0][:, k, c, 1:129]

    XO = 42 * KB
    xh_tm = RV(0, [NCH, D], F32)
    BCT = RV(16 * KB, [4, 512], BF16)
    Bt = RV(20 * KB, [NCH, 2, 128], BF16)
    Esb = RV(22 * KB, [NCH, 96], F32)
    dtt = RV(22 * KB + 1536, [NCH, 32], F32)
    att = RV(24 * KB, [NCH, 32], F32)
    sm = RV(24 * KB + 512, [256], F32)
    zs = RV(26 * KB, [NCH, D], BF16)
    y_a = RV(34 * KB, [NCH, D], BF16)
    xcT = RV(XO, [12, 512], F32)
    S_b = RV(XO, [NCH, D], F32)
    Hs = RV(XO + 18 * KB, [NCH, 2, D], BF16)
    xdA = RV(XO + 34 * KB, [D], BF16)
    xdB = RV(XO + 36 * KB, [D], BF16)
    Msel = RV(XO + 38 * KB, [128], F32)
    rhs_seg = RV(XO, [8, 128], BF16)
    exps = RV(XO + 4 * KB, [8, 128], BF16)
    cbm = RV(XO + 8 * KB, [2, 2, 128], BF16)
    MT = RV(XO + 10 * KB, [2, 16, 128], BF16)
    t1 = RV(XO + 38 * KB + 512, [512], F32)
    t2 = RV(XO + 40 * KB + 512, [512], F32)
    yc = RV(XO + 42 * KB + 512, [D], F32)
    assert XO + 46 * KB + 512 <= 89 * KB
    assert R + 93 * KB <= A.nbytes, (R, A.nbytes)

    class BufSet:
        pass

    BS0 = BufSet()
    BS0.xh_tm, BS0.BCT, BS0.Bt, BS0.dtt, BS0.att, BS0.xcT = xh_tm, BCT, Bt, dtt, att, xcT
    attb = RV(25 * KB + 512, [NCH, 32], BF16)
    BS0.attb = attb

    def ssd_front(ti, bs=None, parts="abc", need_c=True):
        bs = bs or BS0
        xcT_, BCT_ = bs.xcT, bs.BCT
        if "a" in parts:
            for wi in range(3):
                ncol = 512 if (need_c or wi < 2) else 256
                wt = wtile(w_in[:, 1024 + wi * 512:1024 + wi * 512 + ncol], ncol)
                for s in range(ncol // 128):
                    j = wi * 4 + s
                    pp = [PS(), PS()]
                    for half in range(2):
                        for k in range(8):
                            rhs = fpat(HTC[0][:, k, 2 * half, 0:1], [[1, 2 * EXT]])
                            mm(pp[half][:, 0:2 * EXT], wt[:, k, s * 128:(s + 1) * 128], rhs, k == 0, k == 7)
                    for half in range(2):
                        pv = pp[half][:, 0:2 * EXT].rearrange("p (c t) -> p c t", c=2)
                        acc = xcT_[:, j, half * 256:(half + 1) * 256].rearrange("p (c t) -> p c t", c=2)
                        act(acc, pv[:, :, 1:129], AF.Identity, bias=convb[:, j:j + 1], scale=convw[:, j, 1:2])
                        stt(acc, pv[:, :, 0:128], convw[:, j, 0:1], acc, ALU.mult, ALU.add)
                        stt(acc, pv[:, :, 2:130], convw[:, j, 2:3], acc, ALU.mult, ALU.add)
                    if j < 8:
                        if getattr(bs, "xsb", None) is not None:
                            act(bs.xsb[:, j, :], xcT_[:, j, :], AF.Silu)
                        else:
                            act(xcT_[:, j, :], xcT_[:, j, :], AF.Silu)
                    else:
                        act(BCT_[:, j - 8, :], xcT_[:, j, :], AF.Silu)
        if "b" in parts:
            wt = wtile(w_in[:, 2560:2592], 32)
            p = PS()
            for c in range(NCH):
                for k in range(8):
                    mm(p[:, c * 32:(c + 1) * 32], hT_chunk(k, c), wt[:, k, 0:32], k == 0, k == 7)
            tt(bs.dtt, p[:, 0:128].rearrange("p (c h) -> p c h", c=NCH), fpat(dtb, [[0, NCH], [1, 32]]), ALU.add)
            act(bs.dtt, bs.dtt, AF.Exp)
            act(bs.dtt, bs.dtt, AF.Ln, bias=cst[:, 1:2], scale=1.0)
            tt(bs.att, bs.dtt, fpat(aneg, [[0, NCH], [1, 32]]), ALU.mult)
            if getattr(bs, "attb", None) is not None:
                cp(bs.attb, bs.att)
        if "c" in parts:
            for c in range(NCH if getattr(bs, "xsb", None) is not None else 0):
                p = PS()
                pb = p.bitcast(BF16)
                for f in range(8):
                    tr(pb[:, f * 128:(f + 1) * 128], bs.xsb[:, f, c * 128:(c + 1) * 128], identb)
                cp(bs.xh_tm[:, c, :], pb[:, 0:1024], eng="act" if c % 2 else "dve")
            for c in range(0 if getattr(bs, "xsb", None) is not None else NCH):
                for hh in range(2):
                    p = PS()
                    for q in range(4):
                        f = hh * 4 + q
                        tr(p[:, q * 128:(q + 1) * 128], xcT_[:, f, c * 128:(c + 1) * 128], ident)
                    cp(bs.xh_tm[:, c, hh * 512:(hh + 1) * 512], p[:, :], eng="act" if hh else "dve")
            p = PS()
            pb = p.bitcast(BF16)
            for c in range(NCH):
                for g in range(2):
                    i = c * 2 + g
                    tr(pb[:, i * 128:(i + 1) * 128], BCT_[:, g, c * 128:(c + 1) * 128], identb)
            cp(bs.Bt.rearrange("p c g n -> p (c g n)"), pb[:, 0:1024])

    def bc_heads(ap16, nh=16):
        return fpat(ap16, [[1, nh], [0, 64]])

    def v3(ap, nh=16):
        return ap.rearrange("p (h d) -> p h d", h=nh)

    def transposeH_out(Hacc, dst):
        for hh in range(2):
            p = PS()
            for q in range(4):
                f = hh * 4 + q
                tr(p[:, q * 128:(q + 1) * 128], Hacc[:, f * 128:(f + 1) * 128], ident)
            tmp = t1 if hh == 0 else t2
            cp(tmp, p[:, :], eng="act")
            S.dma("sp", dst[hh * 512:(hh + 1) * 512, :].rearrange("(q p) n -> p q n", p=128),
                  tmp.rearrange("p (q n) -> p q n", q=4), "st")

    def loadH(Hacc, src):
        for hh in range(2):
            tmp = t1 if hh == 0 else t2
            S.dma("sp", tmp.rearrange("p (q n) -> p q n", q=4),
                  src[hh * 512:(hh + 1) * 512, :].rearrange("(q p) n -> p q n", p=128), "misc")
            p = PS()
            for q in range(4):
                tr(p[:, q * 128:(q + 1) * 128], tmp[:, q * 128:(q + 1) * 128], ident)
            cp(Hacc[:, hh * 512:(hh + 1) * 512], p[:, :], eng="act")

    def ssd_phaseAB(fchain, bchain):
        finfo = {c: (Hacc, rs, fd) for (c, Hacc, rs, fd) in fchain}
        for c in range(NCH):
            p = PS()
            mm(p[:, 0:16], m_leb, attb[:, c, 0:16])
            mm(p[:, 16:32], m_geb, attb[:, c, 16:32])
            mm(p[:, 32:48], m_gtb, attb[:, c, 0:16])
            mm(p[:, 48:64], m_ltb, attb[:, c, 16:32])
            mm(p[:, 64:96], onesb, attb[:, c, 0:32])
            act(Esb[:, c, :], p[:, 0:96], AF.Exp)
            wd = sm[:, 0:32]
            tt(wd, dtt[:, c, :], Esb[:, c, 32:64], ALU.mult)
            tt(v3(xdA), v3(xh_tm[:, c, :]), bc_heads(wd[:, 0:16]), ALU.mult, eng=PEN if "xdA" in OFF else "dve")
            tt(v3(xdB), v3(xh_tm[:, c, :]), bc_heads(wd[:, 16:32]), ALU.mult, eng=PEN if "xdA" in OFF else "dve")
            Hacc, rs, fd = finfo[c]
            if rs:
                memset(Hacc, 0.0)
            cp(Hs[:, c, 0, :], Hacc, eng="act")
            tt(v3(Hacc), v3(Hacc), bc_heads(Esb[:, c, 64:80]), ALU.mult, eng=PEN if "HA" in OFF else "dve")
            for g in range(2):
                p = PS()
                mm(p[:, :], Bt[:, c, g, :], xdA[:, g * 512:(g + 1) * 512])
                tt(Hacc[:, g * 512:(g + 1) * 512], Hacc[:, g * 512:(g + 1) * 512], p[:, :], ALU.add)
            for g in range(2):
                p = PS()
                mm(p[:, :], Bt[:, c, g, :], xdB[:, g * 512:(g + 1) * 512])
                cp(S_b[:, c, g * 512:(g + 1) * 512], p[:, :], eng="act")
            if fd is not None:
                transposeH_out(Hacc, fd)
        for (c, Hacc, rs, fd) in bchain:
            if rs:
                memset(Hacc, 0.0)
            cp(Hs[:, c, 1, :], Hacc, eng="act")
            tt(v3(Hacc), v3(Hacc), bc_heads(Esb[:, c, 80:96]), ALU.mult, eng=PEN if "HB" in OFF else "dve")
            tt(Hacc, Hacc, S_b[:, c, :], ALU.add, eng=PEN if "HB" in OFF else "dve")
            if fd is not None:
                transposeH_out(Hacc, fd)

    def z_proj():
        for hw in range(2):
            wt = wtile(w_in[:, hw * 512:(hw + 1) * 512], 512)
            for c in range(NCH):
                p = PS()
                for k in range(8):
                    mm(p[:, :], hT_chunk(k, c), wt[:, k, :], k == 0, k == 7)
                act(zs[:, c, hw * 512:(hw + 1) * 512], p[:, :], AF.Silu)

    def ssd_phaseC():
        t3s = [RV(XO + 47 * KB, [512], F32), RV(XO + 49 * KB, [512], F32)]
        for c in range(NCH):
            tok = slice(c * 128, (c + 1) * 128)
            p = PS()
            for g in range(2):
                mm(p[:, g * 128:(g + 1) * 128], BCT[:, g, tok], BCT[:, 2 + g, tok])
            pv = p[:, 0:256].rearrange("p (g i) -> p g i", g=2)
            tt(cbm[:, 0, :, :], pv, fpat(m_le, [[0, 2], [1, 128]]), ALU.mult)
            tt(cbm[:, 1, :, :], pv, fpat(m_ge, [[0, 2], [1, 128]]), ALU.mult)
            tt(v3(xdA), v3(xh_tm[:, c, :]), bc_heads(dtt[:, c, 0:16]), ALU.mult, eng=PEN if "xdC" in OFF else "dve")
            tt(v3(xdB), v3(xh_tm[:, c, :]), bc_heads(dtt[:, c, 16:32]), ALU.mult, eng=PEN if "xdC" in OFF else "dve")
            for d in range(2):
                msk = m_le if d == 0 else m_ge
                lmask = m_gtb if d == 0 else m_ltb
                for hh in range(2):
                    for hl in range(8):
                        col = d * 16 + hh * 8 + hl
                        act(rhs_seg[:, hl, :], msk, AF.Identity, scale=att[:, c, col:col + 1])
                    for q in range(2):
                        p = PS()
                        mm(p[:, :], lmask, rhs_seg[:, q * 4:(q + 1) * 4, :].rearrange("p h i -> p (h i)"))
                        act(exps[:, q * 4:(q + 1) * 4, :].rearrange("p h i -> p (h i)"), p[:, :], AF.Exp)
                    tt(MT[:, d, hh * 8:(hh + 1) * 8, :], exps, fpat(cbm[:, d, hh, :], [[0, 8], [1, 128]]), ALU.mult)
            for g in range(2):
                pd = PS()
                for hl in range(8):
                    h = g * 8 + hl
                    mm(pd[:, hl * 64:(hl + 1) * 64], MT[:, 0, h, :], xdA[:, h * 64:(h + 1) * 64], True, False)
                    mm(pd[:, hl * 64:(hl + 1) * 64], MT[:, 1, h, :], xdB[:, h * 64:(h + 1) * 64], False, True)
                pf = PS()
                mm(pf[:, :], BCT[:, 2 + g, tok], Hs[:, c, 0, g * 512:(g + 1) * 512])
                pbk = PS()
                mm(pbk[:, :], BCT[:, 2 + g, tok], Hs[:, c, 1, g * 512:(g + 1) * 512])
                ycg = yc[:, g * 512:(g + 1) * 512]
                tt(v3(t1, 8), v3(pf[:, :], 8), bc_heads(Esb[:, c, g * 8:g * 8 + 8], 8), ALU.mult)
                tt(v3(t2, 8), v3(pbk[:, :], 8), bc_heads(Esb[:, c, 16 + g * 8:16 + g * 8 + 8], 8), ALU.mult)
                tt(ycg, pd[:, :], t1, ALU.add)
                tt(ycg, ycg, t2, ALU.add, eng=PEN)
                t3 = t3s[g]
                tt(v3(t3, 8), v3(xh_tm[:, c, g * 512:(g + 1) * 512], 8), bc_heads(dskip[:, g * 8:g * 8 + 8], 8), ALU.mult, eng=PEN)
                tt(ycg, ycg, t3, ALU.add, eng=PEN)
            tt(yc, yc, zs[:, c, :], ALU.mult, eng=PEN if "gate" in OFF else "dve")
            st = scr[:, 0:12]
            mv = scr[:, 12:14]
            S.add("dve", lambda e, yc=yc, st=st: e.bn_stats(out=st[:, 0:6], in_=yc[:, 0:512]), [yc], [st[:, 0:6]])
            S.add("dve", lambda e, yc=yc, st=st: e.bn_stats(out=st[:, 6:12], in_=yc[:, 512:1024]), [yc], [st[:, 6:12]])
            S.add("dve", lambda e, st=st, mv=mv: e.bn_aggr(out=mv, in_=st), [st], [mv])
            r = scr[:, 14:15]
            tt(r, mv[:, 0:1], mv[:, 0:1], ALU.mult)
            tt(r, r, mv[:, 1:2], ALU.add)
            act(r, r, AF.Sqrt, bias=cst[:, 2:3], scale=1.0)
            S.add("dve", lambda e, r=r: e.reciprocal(out=r, in_=r), [r], [r])
            act(y_a[:, c, :], yc, AF.Identity, scale=r)

    def make_yaT(y_aT):
        base = ps_ctr[0]
        ps_ctr[0] += 2
        for f in range(8):
            p = PSB[(base + (f % 2)) % 8]
            pb = p.bitcast(BF16)
            for c in range(NCH):
                tr(pb[:, c * 128:(c + 1) * 128], y_a[:, c, f * 128:(f + 1) * 128], identb)
            act(y_aT[:, f, :], pb[:, 0:512], AF.Identity, scale=normw[:, f:f + 1])

    def layer_norm(x, out, g_bc, b_bc):
        st = scr[:, 16:28]
        mv = scr[:, 28:30]
        S.add("dve", lambda e: e.bn_stats(out=st[:, 0:6], in_=x[:, 0:512]), [x], [st[:, 0:6]])
        S.add("dve", lambda e: e.bn_stats(out=st[:, 6:12], in_=x[:, 512:1024]), [x], [st[:, 6:12]])
        S.add("dve", lambda e: e.bn_aggr(out=mv, in_=st), [st], [mv])
        r = scr[:, 30:31]
        act(r, mv[:, 1:2], AF.Sqrt, bias=cst[:, 2:3], scale=1.0)
        S.add("dve", lambda e: e.reciprocal(out=r, in_=r), [r], [r])
        stt(x, x, mv[:, 0:1], g_bc, ALU.subtract, ALU.mult)
        stt(out, x, r, b_bc, ALU.mult, ALU.add)

    def load_ln(gsrc, bsrc):
        S.dma("sp", lnA, rowb(gsrc), "lnp")
        S.dma("sp", lnB, rowb(bsrc), "lnp")

    def sgu(y_bT):
        guT = RV(8 * KB, [8, 512], F32)
        vtmp = RV(24 * KB, [NCH, D], F32)
        v_ln = RV(40 * KB, [NCH, D], BF16)
        for wi in range(2):
            wt = wtile(w_in[:, 2592 + wi * 512:2592 + (wi + 1) * 512], 512)
            for s in range(4):
                fc = wi * 4 + s
                p = PS()
                for k in range(8):
                    mm(p[:, :].rearrange("p (c t) -> p c t", c=NCH), wt[:, k, s * 128:(s + 1) * 128], hT_main(k), k == 0, k == 7)
                act(guT[:, fc, :], p[:, :], AF.Gelu)
        for hw in range(2):
            wt = wtile(w_in[:, 3616 + hw * 512:3616 + (hw + 1) * 512], 512)
            for c in range(NCH):
                p = PS()
                for k in range(8):
                    mm(p[:, :], hT_chunk(k, c), wt[:, k, :], k == 0, k == 7)
                act(vtmp[:, c, hw * 512:(hw + 1) * 512], p[:, :], AF.Gelu)
        load_ln(d_sgug, d_sgub)
        for c in range(NCH):
            layer_norm(vtmp[:, c, :], v_ln[:, c, :], lnA, lnB)
        for c in range(NCH):
            for q in range(2):
                p = PS()
                for gl in range(4):
                    g = q * 4 + gl
                    o = p[:, gl * 128:(gl + 1) * 128]
                    mm(o, v_ln[:, c, g * 128:(g + 1) * 128], wspT[:, g, :], True, False)
                    mm(o, onesb[0:1, :], bsph[0:1, g * 128:(g + 1) * 128], False, False)
                    mm(o, onesb[0:1, :], bspl[0:1, g * 128:(g + 1) * 128], False, True)
                tt(y_bT[:, q * 4:(q + 1) * 4, c * 128:(c + 1) * 128],
                   p[:, :].rearrange("p (g i) -> p g i", g=4),
                   guT[:, q * 4:(q + 1) * 4, c * 128:(c + 1) * 128], ALU.mult)

    def merge(y_aT, y_bT, mergedT):
        mA = RV(8 * KB, [8, 512], F32)
        sg = RV(24 * KB, [512], F32)
        tm = RV(26 * KB, [512], F32)
        for br in range(2):
            src = y_aT if br == 0 else y_bT
            wproj = w_pa if br == 0 else w_pb
            gbase = 4640 + br * 1024
            for hw in range(2):
                wg = wtile(w_in[:, gbase + hw * 512:gbase + (hw + 1) * 512], 512)
                wp = wtile(wproj[:, hw * 512:(hw + 1) * 512], 512)
                for s in range(4):
                    fc = hw * 4 + s
                    pg = PS()
                    for k in range(8):
                        mm(pg[:, :].rearrange("p (c t) -> p c t", c=NCH), wg[:, k, s * 128:(s + 1) * 128], hT_main(k), k == 0, k == 7)
                    pp = PS()
                    for k in range(8):
                        mm(pp[:, :], wp[:, k, s * 128:(s + 1) * 128], src[:, k, :], k == 0, k == 7)
                    act(sg, pg[:, :], AF.Sigmoid)
                    if br == 0:
                        tt(mA[:, fc, :], pp[:, :], sg, ALU.mult)
                    else:
                        tt(tm, pp[:, :], sg, ALU.mult)
                        tt(mergedT[:, fc, :], tm, mA[:, fc, :], ALU.add)

    def resid_ln(x_tm, c, pbanks, gate_bc, out_ap):
        for hw in range(2):
            tm = RV(24 * KB + hw * 2 * KB, [512], F32)
            tt(tm, pbanks[hw][:, :], gate_bc[:, hw * 512:(hw + 1) * 512], ALU.mult)
            xs = x_tm[:, c, hw * 512:(hw + 1) * 512]
            stt(xs, xs, ALPHA, tm, ALU.mult, ALU.add)
        layer_norm(x_tm[:, c, :], out_ap, lnA, lnB)

    def out_proj(x_tm, mergedT, cond):
        load_ln(d_ln1g, d_ln1b)
        wts = [wtile(w_out[:, hw * 512:(hw + 1) * 512], 512) for hw in range(2)]
        for c in range(NCH):
            pb = []
            for hw in range(2):
                p = PS()
                for k in range(8):
                    mm(p[:, :], mergedT[:, k, c * 128:(c + 1) * 128], wts[hw][:, k, :], k == 0, k == 7)
                pb.append(p)
            resid_ln(x_tm, c, pb, gates[:, cond, 0, :], x_tm[:, c, :])

    def ffn(x_tm, cond, out_dram):
        f1T = RV(28 * KB, [32, 512], BF16)
        rl = [RV(16 * KB, [512], F32), RV(18 * KB, [512], F32)]
        for wi in range(8):
            wt = wtile(w_ff1[:, wi * 512:(wi + 1) * 512], 512)
            for s in range(4):
                fc = wi * 4 + s
                p = PS()
                for k in range(8):
                    mm(p[:, :].rearrange("p (c t) -> p c t", c=NCH), wt[:, k, s * 128:(s + 1) * 128], hT_main(k), k == 0, k == 7)
                r = rl[fc % 2]
                act(r, p[:, :], AF.Relu)
                tt(f1T[:, fc, :], r, r, ALU.mult)
        load_ln(d_ln2g, d_ln2b)
        pbs = [[None, None] for _ in range(NCH)]
        for hw in range(2):
            banks = [PS() for _ in range(NCH)]
            for kg in range(4):
                wt = wtile(w_ff2[kg * 1024:(kg + 1) * 1024, hw * 512:(hw + 1) * 512], 512)
                for c in range(NCH):
                    for k in range(8):
                        mm(banks[c][:, :], f1T[:, kg * 8 + k, c * 128:(c + 1) * 128], wt[:, k, :],
                           kg == 0 and k == 0, kg == 3 and k == 7)
            for c in range(NCH):
                tm = RV(24 * KB + hw * 2 * KB, [512], F32)
                tt(tm, banks[c][:, :], gates[:, cond, 1, hw * 512:(hw + 1) * 512], ALU.mult)
                xs = x_tm[:, c, hw * 512:(hw + 1) * 512]
                stt(xs, xs, ALPHA, tm, ALU.mult, ALU.add)
        for c in range(NCH):
            layer_norm(x_tm[:, c, :], x_tm[:, c, :], lnA, lnB)
            S.dma("sp", out_dram[c * 128:(c + 1) * 128, :], x_tm[:, c, :], "st")

    def full_tile(ti, cond, fchain, bchain, out_dram):
        x_tm = RV(64 * KB, [NCH, D], F32)
        xh_rows = None
        chk("t%d:x" % ti)
        make_hT_fm(ti, cond, RV(0, [8, NCH * EXT], F32), RV(17 * KB, [NCH * EXT], BF16)[0:64, :])
        chk("t%d:hT" % ti)
        ssd_front(ti)
        chk("t%d:front" % ti)
        ssd_phaseAB(fchain, bchain)
        chk("t%d:AB" % ti)
        z_proj()
        chk("t%d:z" % ti)
        ssd_phaseC()
        chk("t%d:C" % ti)
        y_aT = RV(0, [8, 512], BF16)
        make_yaT(y_aT)
        chk("t%d:yaT" % ti)
        y_bT = RV(48 * KB, [8, 512], BF16)
        sgu(y_bT)
        chk("t%d:sgu" % ti)
        mergedT = RV(56 * KB, [8, 512], BF16)
        merge(y_aT, y_bT, mergedT)
        chk("t%d:merge" % ti)
        load_x(ti, x_tm, xh_rows, False)
        if ti == 0:
            mod_tiles([4, 5])
        out_proj(x_tm, mergedT, cond)
        chk("t%d:outp" % ti)
        if ti == 0:
            mod_tiles([6, 7, 8, 9])
        make_hT(x_tm, None, cond, 2, 3, ti, False)
        chk("t%d:h2T" % ti)
        if ti == 0:
            mod_tiles([10, 11])
        ffn(x_tm, cond, out_dram)
        chk("t%d:ffn" % ti)

    PX = BufSet()
    PX.xT_st = RV(0, [8, NCH * EXT], F32)
    PX.selT = RV(16 * KB + 256, [NCH * EXT], BF16)[0:64, :]
    PX.x_tm = RV(0, [NCH, D], F32)
    PX.sm = RV(18 * KB + 512, [448], F32)
    PX.xcT = RV(20 * KB + 512, [10, 512], F32)
    PX.BCT = RV(40 * KB + 512, [2, 512], BF16)
    PX.xdA = RV(42 * KB + 512, [D], BF16)
    PX.Msel = RV(44 * KB + 512, [128], BF16)
    PX.aselb = RV(20 * KB + 256, [NCH, 16], BF16)
    hT2 = RV(45 * KB, [8, NCH, EXT], BF16)
    PX.xsb = RV(61 * KB + 512, [8, 512], BF16)
    PSET = []
    for i in range(2):
        b = BufSet()
        o = 53 * KB + 512 + i * 19 * KB
        b.xh_tm = RV(o, [NCH, D], BF16)
        b.xsb = PX.xsb
        b.Bt = RV(o + 16 * KB, [NCH, 2, 128], BF16)
        b.dtt = RV(o + 18 * KB, [NCH, 32], F32)
        b.att = RV(o + 18 * KB + 512, [NCH, 32], F32)
        b.xcT, b.BCT = PX.xcT, PX.BCT
        b.hT = hT if i == 0 else hT2
        PSET.append(b)
    PX.xdB2 = RV(91 * KB + 512, [D], BF16)
    assert R + 94 * KB <= A.nbytes, (R, A.nbytes)

    def prefix_front(ti, part):
        bs = PSET[ti % 2]
        HTC[0] = bs.hT
        if part == 1:
            make_hT_fm(ti, 1, PX.xT_st, PX.selT)
        elif part == 2:
            ssd_front(ti, bs, "a", need_c=False)
        else:
            ssd_front(ti, bs, "bc")
        HTC[0] = hT

    def prefix_slots(ti, chunks, Hc):
        if chunks[0] != 0:
            return
        bs = PSET[ti % 2]
        sm = PX.sm
        Msel_ = PX.Msel
        sf = flags[:, ti * 16 + 2:ti * 16 + 3]
        sb = flags[:, ti * 16 + 3:ti * 16 + 4]
        asel = PX.aselb
        dsel = sm[:, 64:128].rearrange("p (c h) -> p c h", c=NCH)
        tmp = sm[:, 128:192].rearrange("p (c h) -> p c h", c=NCH)
        ts(tmp, bs.att[:, :, 16:32], sb, ALU.mult)
        stt(asel, bs.att[:, :, 0:16], sf, tmp, ALU.mult, ALU.add)
        ts(tmp, bs.dtt[:, :, 16:32], sb, ALU.mult)
        stt(dsel, bs.dtt[:, :, 0:16], sf, tmp, ALU.mult, ALU.add)
        ts(Msel_, m_lt, sb, ALU.mult)
        stt(Msel_, m_gt, sf, Msel_, ALU.mult, ALU.add)
        p = PS()
        for c in range(NCH):
            mm(p[:, c * 32:c * 32 + 16], Msel_, asel[:, c, :])
            mm(p[:, c * 32 + 16:c * 32 + 32], onesb, asel[:, c, :])
        ex = sm[:, 192:320].rearrange("p (c h) -> p c h", c=NCH)
        act(ex, p[:, 0:128].rearrange("p (c h) -> p c h", c=NCH), AF.Exp)
        wd = sm[:, 320:384].rearrange("p (c h) -> p c h", c=NCH)
        tt(wd, dsel, ex[:, :, 0:16], ALU.mult)
        pc = sm[:, 384:400]
        cp(pc, ex[:, 3, 16:32])
        tt(wd[:, 2, :], wd[:, 2, :], pc, ALU.mult)
        tt(pc, pc, ex[:, 2, 16:32], ALU.mult)
        tt(wd[:, 1, :], wd[:, 1, :], pc, ALU.mult)
        tt(pc, pc, ex[:, 1, 16:32], ALU.mult)
        tt(wd[:, 0, :], wd[:, 0, :], pc, ALU.mult)
        tt(pc, pc, ex[:, 0, 16:32], ALU.mult)
        banks = [PS(), PS()]
        xds = [PX.xdA, PX.xdB2]
        for c in range(NCH):
            xd = xds[c % 2]
            tt(v3(xd), v3(bs.xh_tm[:, c, :]), bc_heads(wd[:, c, :]), ALU.mult, eng=PEN if (("xdP" in OFF and c % 2 == 1) or "xdPall" in OFF) else "dve")
            for g in range(2):
                mm(banks[g][:, :], bs.Bt[:, c, g, :], xd[:, g * 512:(g + 1) * 512], c == 0, c == NCH - 1)
        tt(v3(Hc), v3(Hc), bc_heads(pc), ALU.mult, eng=PEN if "HcP" in OFF else "dve")
        for g in range(2):
            gs = slice(g * 512, (g + 1) * 512)
            tt(Hc[:, gs], Hc[:, gs], banks[g][:, :], ALU.add)

    def switch_point(qp, Hc, Hinit, Hsave):
        w = swf[:, qp:qp + 1]
        tmp = PX.x_tm[:, 0, :]
        stt(Hsave, Hc, w, Hsave, ALU.mult, ALU.add)
        tt(tmp, Hinit, Hc, ALU.subtract)
        stt(Hc, tmp, w, Hc, ALU.mult, ALU.add)

    def prefix_all():
        memset(Hf0, 0.0)
        for part in (1, 2, 3):
            prefix_front(1, part)
        for ti in range(1, 8):
            nxt = ti + 1 if ti < 7 else None
            if ti in (1, 3, 5):
                switch_point((ti - 1) // 2, Hf, Hb, Hf0)
            if ti == 7:
                switch_point(3, Hf, Hb, Hf0)
                cp(Hb, Hf0)
            Hc = Hb if ti == 7 else Hf
            if nxt:
                prefix_front(nxt, 1)
            prefix_slots(ti, (0, 1), Hc)
            if nxt:
                prefix_front(nxt, 2)
            prefix_slots(ti, (2, 3), Hc)
            if nxt:
                prefix_front(nxt, 3)
            chk("t%d" % ti)

    DUMPABLE = {
        "modcol": (modcol.rearrange("p a b c -> p (a b c)"), 64),
        "gates": (gates.rearrange("p a b c -> p (a b c)"), 4 * D),
        "hT": (hT.rearrange("p a b c -> p (a b c)"), 8 * NCH * EXT),
        "xh_tm": (xh_tm.rearrange("p a b -> p (a b)"), NCH * D),
        "BCT": (BCT.rearrange("p a b -> p (a b)"), 4 * 512),
        "dtt": (dtt.rearrange("p a b -> p (a b)"), NCH * 32),
        "Esb": (Esb.rearrange("p a b -> p (a b)"), NCH * 96),
        "Hs": (Hs.rearrange("p a b c -> p (a b c)"), NCH * 2 * D),
        "zs": (zs.rearrange("p a b -> p (a b)"), NCH * D),
        "y_a": (y_a.rearrange("p a b -> p (a b)"), NCH * D),
        "y_aT": (RV(0, [8 * 512], BF16), 8 * 512),
        "y_bT": (RV(48 * KB, [8 * 512], BF16), 8 * 512),
        "mergedT": (RV(56 * KB, [8 * 512], BF16), 8 * 512),
        "x_tm": (RV(0, [NCH * D], F32), NCH * D),
        "Hf": (Hf, D), "Hb": (Hb, D), "Hf0": (Hf0, D),
        "Epos": (Epos[0:64, :], 512),
        "cbm": (cbm.rearrange("p a b c -> p (a b c)"), 512),
        "MT": (MT.rearrange("p a b c -> p (a b c)"), 2 * 16 * 128),
        "xdA": (xdA, D), "xdB": (xdB, D), "yc": (yc, D),
        "exps": (exps.rearrange("p a b -> p (a b)"), 1024),
        "rhs_seg": (rhs_seg.rearrange("p a b -> p (a b)"), 1024),
        "att": (att.rearrange("p a b -> p (a b)"), NCH * 32),
        "scr": (scr, 64),
    }

    def dump(name, ap, nfree):
        d = nc.dram_tensor("dbg_" + name, [int(list(ap.ap)[0][1]), nfree], ap.dtype, kind="ExternalOutput").ap()
        S.dma("sp", d, ap, "st")

    try:
        mod_init()
        mod_tiles([0, 1, 2, 3])
        chk("mod")
        full_tile(0, 0,
                  fchain=[(0, Hf, True, None), (1, Hf, False, o_nsf[0]), (2, Hf, True, None), (3, Hf, False, o_nsf[1])],
                  bchain=[(3, Hb, True, None), (2, Hb, False, o_nsb[1]), (1, Hb, True, None), (0, Hb, False, o_nsb[0])],
                  out_dram=o_yp)
        build_pos()
        loadH(Hf, h0f)
        loadH(Hb, h0b)
        chk("pos")
        prefix_all()
        full_tile(8, 1,
                  fchain=[(c, Hb, False, None) for c in range(4)],
                  bchain=[(c, Hf, False, None) for c in (3, 2, 1, 0)],
                  out_dram=o_ys[512:1024, :])
        full_tile(9, 1,
                  fchain=[(c, Hf0, False, None) for c in range(4)],
                  bchain=[(c, Hf, False, None) for c in (3, 2, 1, 0)],
                  out_dram=o_ys[0:512, :])
    except _Stop:
        pass
    if dumps:
        for nm in dumps:
            dump(nm, *DUMPABLE[nm])
    S.emit(final_wait_keys=["st"] if "st" in S.semkeys else [])
    es.close()
    return nc


_NC_CACHE = {}


def _host_inputs(inp):
    f32 = np.float32
    g = lambda k: np.asarray(inp[k], dtype=f32)
    x_prompt, x_sample = g("x_prompt"), g("x_sample")
    shared = {
        "w_ada": np.ascontiguousarray(g("w_ada")[0]),
        "w_in": np.ascontiguousarray(g("w_in")[0]),
        "w_pa": np.ascontiguousarray(g("w_proj_a")[0]),
        "w_pb": np.ascontiguousarray(g("w_proj_b")[0]),
        "w_out": np.ascontiguousarray(g("w_out")[0]),
        "w_ff1": np.ascontiguousarray(g("w_ff1")[0]),
        "w_ff2": np.ascontiguousarray(g("w_ff2")[0]),
        "b_ada_col": np.ascontiguousarray(g("b_ada")[0].reshape(48, 128).T),
        "b_ada_row": np.ascontiguousarray(g("b_ada")[0].reshape(1, -1)),
        "convw": np.ascontiguousarray(g("conv_w")[0].reshape(3, 12, 128).transpose(2, 1, 0).reshape(128, 36)),
        "convb": np.ascontiguousarray(g("conv_b")[0].reshape(12, 128).T),
        "dtb": np.concatenate([g("dt_bias_fwd")[0], g("dt_bias_bwd")[0]]).reshape(1, 32),
        "alog": np.concatenate([g("a_log_fwd")[0], g("a_log_bwd")[0]]).reshape(1, 32),
        "dskip": g("d_skip")[0].reshape(1, 16),
        "normw_col": np.ascontiguousarray(g("ssd_norm_w")[0].reshape(8, 128).T),
        "sgu_g": g("sgu_ln_g")[0].reshape(1, -1), "sgu_b": g("sgu_ln_b")[0].reshape(1, -1),
        "ln1_g": g("ln1_g")[0].reshape(1, -1), "ln1_b": g("ln1_b")[0].reshape(1, -1),
        "ln2_g": g("ln2_g")[0].reshape(1, -1), "ln2_b": g("ln2_b")[0].reshape(1, -1),
        "wspT": np.ascontiguousarray(g("w_spatial")[0].transpose(2, 0, 1).reshape(128, 1024)),
        "bsp": g("b_spatial")[0].reshape(1, 1024),
    }
    idx = np.arange(128)
    shared["ident"] = np.eye(128, dtype=f32)
    shared["m_le"] = (idx[:, None] <= idx[None, :]).astype(f32)
    shared["m_ge"] = (idx[:, None] >= idx[None, :]).astype(f32)
    shared["m_gt"] = (idx[:, None] > idx[None, :]).astype(f32)
    shared["m_lt"] = (idx[:, None] < idx[None, :]).astype(f32)
    shared["jidx"] = np.broadcast_to(np.arange(256, dtype=f32)[None, :], (64, 256)).copy()
    shared["pidx"] = np.arange(64, dtype=f32).reshape(64, 1)
    p64 = np.arange(64)
    shared["selcol"] = (p64[:, None] == (idx[None, :] % 64)).astype(f32)
    hcols = np.array([63, 0] * 4)
    shared["selcol_h"] = (p64[:, None] == hcols[None, :]).astype(f32)
    ecols = np.concatenate([[63], np.arange(128) % 64, [0]])
    shared["selc"] = (p64[:, None] == ecols[None, :]).astype(f32)
    zero_row = np.zeros((D,), f32)
    c_ctx, c = g("c_ctx"), g("c")
    sf_, sb_ = g("state_ssd_fwd"), g("state_ssd_bwd")
    in_maps = []
    for cid in range(8):
        b, q = cid // 4, cid % 4
        xall = np.zeros((NT, 520, D), f32)
        flags = np.zeros((NT, 4, 4), f32)
        selrow = np.zeros((NT - 1, 64, 512), f32)
        selrow_h = np.zeros((NT - 1, 64, 8), f32)
        for s in range(2):
            seq = x_prompt[2 * cid + s]
            for cc in range(2):
                ch = s * 2 + cc
                xall[0, ch * 128:(ch + 1) * 128] = seq[cc * 128:(cc + 1) * 128]
                if cc == 1:
                    xall[0, 512 + 2 * ch] = seq[127]
                    flags[0, ch, 0] = 1.0
                else:
                    xall[0, 512 + 2 * ch + 1] = seq[128]
                    flags[0, ch, 1] = 1.0
        xs = x_sample[b]
        slots = [(gc, 1.0, 0.0) for gc in range(0, 8 * q)] + [(gc, 0.0, 1.0) for gc in range(31, 8 * q + 7, -1)]
        assert len(slots) == 24
        slots += [(8 * q + j, 1.0, 0.0) for j in range(4)]
        slots += [(8 * q + 4 + j, 0.0, 0.0) for j in range(4)]
        slots += [(8 * q + j, 0.0, 0.0) for j in range(4)]
        for si, (gc, sf, sb) in enumerate(slots):
            ti, ch = 1 + si // 4, si % 4
            t0 = gc * 128
            xall[ti, ch * 128:(ch + 1) * 128] = xs[t0:t0 + 128]
            flags[ti, ch, 2], flags[ti, ch, 3] = sf, sb
            tok = t0 + idx
            selrow[ti - 1, tok // 64, ch * 128 + idx] = 1.0
            if t0 > 0:
                xall[ti, 512 + 2 * ch] = xs[t0 - 1]
                flags[ti, ch, 0] = 1.0
                selrow_h[ti - 1, (t0 - 1) // 64, 2 * ch] = 1.0
            if t0 + 128 < 4096:
                xall[ti, 512 + 2 * ch + 1] = xs[t0 + 128]
                flags[ti, ch, 1] = 1.0
                selrow_h[ti - 1, (t0 + 128) // 64, 2 * ch + 1] = 1.0
        ccol = np.stack([c_ctx.reshape(8, 128).T, c[b].reshape(8, 128).T], axis=-1).reshape(128, 16)
        ext = np.empty((NT, NCH, EXT, D), f32)
        ext[:, :, 1:129] = xall[:, :512].reshape(NT, NCH, 128, D)
        ext[:, :, 0] = xall[:, 512:520:2]
        ext[:, :, 129] = xall[:, 513:520:2]
        xTe = np.ascontiguousarray(ext.reshape(NT, NCH * EXT, D).transpose(0, 2, 1))
        selT = np.zeros((NT - 1, 64, NCH, EXT), f32)
        selT[:, :, :, 1:129] = selrow.reshape(NT - 1, 64, NCH, 128)
        selT[:, :, :, 0] = selrow_h[:, :, 0::2]
        selT[:, :, :, 129] = selrow_h[:, :, 1::2]
        m = dict(shared)
        m.update({
            "xres": np.ascontiguousarray(xall[[0, 8, 9], :512]),
            "xTe": xTe,
            "selT": selT.reshape(NT - 1, 64, NCH * EXT),
            "flags": np.ascontiguousarray(np.broadcast_to(flags.reshape(1, -1), (128, NT * 16))),
            "swf": np.ascontiguousarray(np.broadcast_to((np.arange(4) == q).astype(f32).reshape(1, 4), (128, 4))),
            "selrow": selrow, "selrow_h": selrow_h,
            "ccol": np.ascontiguousarray(ccol),
            "h0f": np.ascontiguousarray(sf_[b, 0].reshape(D, 128)),
            "h0b": np.ascontiguousarray(sb_[b, 0].reshape(D, 128)),
        })
        in_maps.append(m)
    return in_maps


def kernel(**inputs):
    if "nc" not in _NC_CACHE:
        _NC_CACHE["nc"] = build_program()
    nc = _NC_CACHE["nc"]
    in_maps = _host_inputs(inputs)
    res = run_bass_kernel_spmd(nc, in_maps, core_ids=list(range(8)))
    r = res.results
    y_prompt = np.zeros((16, 256, D), np.float32)
    y_sample = np.zeros((2, 4096, D), np.float32)
    nsf = np.zeros((16, 1, 16, 64, 128), np.float32)
    nsb = np.zeros((16, 1, 16, 64, 128), np.float32)
    for cid in range(8):
        b, q = cid // 4, cid % 4
        yp = np.asarray(r[cid]["yp"]).reshape(2, 256, D)
        y_prompt[2 * cid:2 * cid + 2] = yp
        y_sample[b, q * 1024:(q + 1) * 1024] = np.asarray(r[cid]["ys"])
        nsf[2 * cid:2 * cid + 2, 0] = np.asarray(r[cid]["nsf"]).reshape(2, 16, 64, 128)
        nsb[2 * cid:2 * cid + 2, 0] = np.asarray(r[cid]["nsb"]).reshape(2, 16, 64, 128)
    return (y_prompt, y_sample, nsf, nsb)
```

```python
import math
import numpy as np
from contextlib import ExitStack
import concourse.bass as bass
import concourse.mybir as mybir
from concourse.bass_utils import run_bass_kernel_spmd

F32 = mybir.dt.float32
BF16 = mybir.dt.bfloat16
AF = mybir.ActivationFunctionType
ALU = mybir.AluOpType
I32 = mybir.dt.int32
DT_SIZE = {F32: 4, BF16: 2, I32: 4}

D = 1024
NCH = 4
EXT = 130
NT = 10
DFF = 4096
DIN = 6688
EPS = 1e-5
ALPHA = 2.0 ** 0.25
NRING = 4


class Op:
    __slots__ = ("eng", "fn", "deps", "idx", "needs_sig", "sigval", "dma", "semkey", "semval", "chain",
                 "size", "pos", "inexact", "sync_same", "cost", "nbytes", "psrc", "stage", "order_only", "actset")

    def __init__(self, eng, fn):
        self.eng = eng
        self.fn = fn
        self.deps = set()
        self.needs_sig = False
        self.sigval = 0
        self.dma = False
        self.semkey = None
        self.semval = 0
        self.chain = True
        self.size = 64
        self.pos = 0
        self.inexact = set()
        self.sync_same = set()
        self.order_only = set()
        self.actset = None
        self.cost = None
        self.nbytes = 0
        self.psrc = False


class Sched:
    ENGS = ["sp", "act", "dve", "pool", "pe"]
    BLK = {"sp": "sync", "act": "scalar", "dve": "vector", "pool": "gpsimd", "pe": "tensor"}

    def __init__(self, nc):
        self.nc = nc
        self.ops = []
        self.regions = {}
        self.semkeys = {}
        self.engpos = {}
        self.stage = "init"
        self.prio_mode = "blevel"

    def _region(self, ap):
        t = ap.tensor
        if type(t).__name__.startswith("DRam"):
            return None
        if type(t).__name__.startswith("PSum"):
            return (t.name, 0, 2048, True)
        esz = DT_SIZE[ap.dtype]
        pstride = 1
        for d in list(t.shape)[1:]:
            pstride *= int(d)
        lo = int(ap.offset) % pstride
        hi = lo + 1
        for step, cnt in list(ap.ap)[1:]:
            hi += (int(cnt) - 1) * abs(int(step))
        return (t.name, lo * esz, hi * esz, False)

    def _access(self, reg, op, is_write):
        name, lo, hi, psum = reg
        if psum:
            is_write = True
        lst = self.regions.get(name, [])
        new = []
        for rec in lst:
            rlo, rhi, rop, rw = rec
            if rlo < hi and lo < rhi:
                if (is_write or rw) and rop is not op:
                    op.deps.add(rop)
                    if rw and not (rlo == lo and rhi == hi):
                        op.inexact.add(rop)
                if is_write and lo <= rlo and rhi <= hi:
                    continue
                if (not is_write) and (not rw) and rop.eng == op.eng and lo <= rlo and rhi <= hi and not rop.dma:
                    if rop is not op:
                        op.deps.add(rop)
                        op.order_only.add(rop)
                    continue
            new.append(rec)
        new.append((lo, hi, op, is_write))
        self.regions[name] = new

    def add(self, eng, fn, reads=(), writes=(), dma=False, semkey=None, chain=True, cost=None, actset=None):
        op = Op(eng, fn)
        op.idx = len(self.ops)
        op.cost = cost
        op.actset = actset
        op.stage = self.stage
        sz = 64
        for ap in list(reads) + list(writes):
            n = 1
            for step, cnt in list(ap.ap)[1:]:
                n *= int(cnt)
            sz = max(sz, n)
            if type(ap.tensor).__name__.startswith("PSum"):
                op.psrc = True
        op.size = sz
        if dma:
            ap = writes[0]
            n = int(list(ap.ap)[0][1])
            for step, cnt in list(ap.ap)[1:]:
                n *= int(cnt)
            op.nbytes = n * 4
        op.pos = self.engpos.get(eng, 0)
        self.engpos[eng] = op.pos + sz
        for ap in reads:
            r = self._region(ap)
            if r:
                self._access(r, op, False)
        for ap in writes:
            r = self._region(ap)
            if r:
                self._access(r, op, True)
        if dma:
            op.dma = True
            op.semkey = semkey
            op.chain = chain
            k = self.semkeys.setdefault(semkey, {"count": 0, "last": None})
            if chain and k["last"] is not None:
                op.deps.add(k["last"])
            k["count"] += 16
            k["last"] = op
            op.semval = k["count"]
        self.ops.append(op)
        return op

    def dma(self, eng, out, in_, semkey, chain=True):
        return self.add(eng, lambda e: e.dma_start(out=out, in_=in_), reads=[in_], writes=[out],
                        dma=True, semkey=semkey, chain=chain)

    def _opcost(self, op):
        if op.cost is not None:
            return op.cost
        if op.dma:
            return 1.0 if op.eng == "pool" else 0.15
        if op.eng == "dve":
            return op.size / 960.0 + (0.2 if op.psrc else 0.157)
        if op.eng == "act":
            return op.size / 1400.0 + 0.22
        return op.size * 2.2 / 1000.0 + 0.15

    def list_schedule(self):
        import heapq
        ops = self.ops
        n = len(ops)
        succ = {}
        indeg = {}
        for op in ops:
            indeg[op] = len(op.deps)
            for d in op.deps:
                succ.setdefault(d, []).append(op)
        blevel = {}
        if self.prio_mode == "blevel":
            for op in reversed(ops):
                c = self._opcost(op) + ((op.nbytes / 280e3 + 2.0) if op.dma else 0.0)
                b = 0.0
                for sop in succ.get(op, ()):
                    if blevel[sop] > b:
                        b = blevel[sop]
                blevel[op] = b + c
        fin = {}
        self.sim_start = {}
        eng_free = {e: 0.0 for e in self.ENGS}
        dma_free = [0.0]
        ready = {e: [] for e in self.ENGS}
        for op in ops:
            if indeg[op] == 0:
                heapq.heappush(ready[op.eng], (0.0, op.idx, op))
        order = []
        cur_set = [None]
        LAT = 0.4
        while len(order) < n:
            best = None
            for e in self.ENGS:
                h = ready[e]
                if not h:
                    continue
                rt, idx, op = h[0]
                st = max(rt, eng_free[e])
                if best is None or (st, idx) < (best[0], best[1]):
                    best = (st, idx, e)
            st, idx, e = best
            h = ready[e]
            cand = []
            while h and h[0][0] <= st + 1e-9:
                cand.append(heapq.heappop(h))
            if self.prio_mode == "blevel":
                if e == "act":
                    cand.sort(key=lambda t: (t[2].actset is not None and t[2].actset != cur_set[0], -blevel[t[2]], t[1]))
                else:
                    cand.sort(key=lambda t: (-blevel[t[2]], t[1]))
            else:
                cand.sort(key=lambda t: t[1])
            rt, idx, op = cand[0]
            for c in cand[1:]:
                heapq.heappush(h, c)
            c = self._opcost(op)
            if e == "act" and op.actset is not None and op.actset != cur_set[0]:
                c += 1.3
                cur_set[0] = op.actset
            eng_free[e] = st + c
            if op.dma:
                t0 = max(st + c, dma_free[0])
                dur = op.nbytes / 280e3
                dma_free[0] = t0 + dur
                fin[op] = t0 + dur + 2.0
            else:
                fin[op] = st + c
            order.append(op)
            self.sim_start[op] = st
            for sop in succ.get(op, ()):
                indeg[sop] -= 1
                if indeg[sop] == 0:
                    rt2 = 0.0
                    for d in sop.deps:
                        lat = 0.0 if (d.eng == sop.eng and not d.dma) else LAT
                        rt2 = max(rt2, fin[d] + lat)
                    heapq.heappush(ready[sop.eng], (rt2, sop.idx, sop))
        self.ops = order
        self.model_time = max(fin.values())
        pos = {e: 0 for e in self.ENGS}
        for i, op in enumerate(order):
            op.idx = i
            op.pos = pos[op.eng]
            pos[op.eng] += op.size

    def emit(self, final_wait_keys=(), reorder=True):
        nc = self.nc
        if reorder:
            self.list_schedule()
        for op in self.ops:
            for d in op.deps:
                if d.eng != op.eng:
                    d.needs_sig = True
                elif op.eng in ("act", "dve", "pool") and not d.dma and d not in op.order_only:
                    gap = op.pos - (d.pos + d.size)
                    if gap >= 512:
                        continue
                    if d.size >= 512 and d not in op.inexact:
                        continue
                    d.needs_sig = True
                    op.sync_same.add(d)
        counters = {e: 0 for e in self.ENGS}
        for op in self.ops:
            if not op.dma and op.needs_sig:
                counters[op.eng] += 1
                op.sigval = counters[op.eng]
        byeng = {e: [] for e in self.ENGS}
        for op in self.ops:
            byeng[op.eng].append(op)
        with ExitStack() as es:
            engsem = {e: es.enter_context(nc.semaphore("sem_" + e)) for e in self.ENGS}
            dmasem = {k: es.enter_context(nc.semaphore("dsem_%d" % i)) for i, k in enumerate(self.semkeys)}
            block = es.enter_context(nc.Block())
            for eng in self.ENGS:
                def body(e, eng=eng):
                    waited = {}
                    for op in byeng[eng]:
                        need = {}
                        for d in op.deps:
                            if d.dma:
                                sem = dmasem[d.semkey]
                                val = d.semval if d.chain else self.semkeys[d.semkey]["count"]
                                key = "d_" + str(d.semkey)
                            else:
                                if d.eng == eng and d not in op.sync_same:
                                    continue
                                sem = engsem[d.eng]
                                val = d.sigval
                                key = "e_" + d.eng
                            if waited.get(key, 0) >= val:
                                continue
                            if need.get(key, (None, 0))[1] < val:
                                need[key] = (sem, val)
                        for key, (sem, val) in need.items():
                            e.wait_ge(sem, val)
                            waited[key] = val
                        ins = op.fn(e)
                        if op.dma:
                            ins.then_inc(dmasem[op.semkey], 16)
                        elif op.needs_sig:
                            ins.then_inc(engsem[eng], 1)
                    if eng == "sp":
                        for k in final_wait_keys:
                            e.wait_ge(dmasem[k], self.semkeys[k]["count"])
                getattr(block, self.BLK[eng])(body)


class Arena:
    def __init__(self, nc, es, name, nbytes):
        self.nbytes = nbytes
        self.t32 = es.enter_context(nc.sbuf_tensor(name, [128, nbytes // 4], F32))
        self.t16 = self.t32.bitcast(BF16)
        assert self.t16.name == self.t32.name
        self.off = 0

    def view(self, off_bytes, shape_free, dtype):
        n = 1
        for d in shape_free:
            n *= d
        esz = DT_SIZE[dtype]
        assert off_bytes % 4 == 0
        assert off_bytes + n * esz <= self.nbytes, ("arena overflow", off_bytes, n * esz, self.nbytes)
        base = self.t32 if dtype == F32 else self.t16
        o = off_bytes // esz
        ap = base[:, o:o + n]
        if len(shape_free) == 1:
            return ap
        names = " ".join("d%d" % i for i in range(len(shape_free)))
        kw = {"d%d" % i: shape_free[i] for i in range(len(shape_free))}
        return ap.rearrange("p (%s) -> p %s" % (names, names), **kw)

    def alloc(self, shape_free, dtype):
        n = 1
        for d in shape_free:
            n *= d
        nb = (n * DT_SIZE[dtype] + 3) // 4 * 4
        v = self.view(self.off, shape_free, dtype)
        self.off += nb
        return v


def fpat(ap, pattern_free, extra_off=0):
    p = list(ap.ap)[0]
    return bass.AP(ap.tensor, int(ap.offset) + extra_off,
                   [[int(p[0]), int(p[1])]] + [[int(a), int(b)] for a, b in pattern_free])


class _Stop(Exception):
    pass


OFFLOAD = {"xdP", "HcP"}


def build_program(upto=None, dumps=None):
    nc = bass.Bass("TRN2", target_bir_lowering=False)
    dbg_list = []

    def chk(name):
        S.stage = name + "+"
        if upto == name:
            raise _Stop()

    def din(name, shape):
        return nc.dram_tensor(name, list(shape), F32, kind="ExternalInput").ap()

    def dout(name, shape):
        return nc.dram_tensor(name, list(shape), F32, kind="ExternalOutput").ap()

    xres = din("xres", [3, 512, D])
    xTe = din("xTe", [NT, D, NCH * EXT])
    d_selT = din("selT", [NT - 1, 64, NCH * EXT])
    d_selc = din("selc", [64, EXT])
    h0f = din("h0f", [D, 128])
    h0b = din("h0b", [D, 128])
    w_ada = din("w_ada", [D, 6 * D])
    w_in = din("w_in", [D, DIN])
    w_pa = din("w_pa", [D, D])
    w_pb = din("w_pb", [D, D])
    w_out = din("w_out", [D, D])
    w_ff1 = din("w_ff1", [D, DFF])
    w_ff2 = din("w_ff2", [DFF, D])
    d_badacol = din("b_ada_col", [128, 48])
    d_badarow = din("b_ada_row", [1, 6 * D])
    d_ccol = din("ccol", [128, 16])
    d_convw = din("convw", [128, 36])
    d_convb = din("convb", [128, 12])
    d_dtb = din("dtb", [1, 32])
    d_alog = din("alog", [1, 32])
    d_dskip = din("dskip", [1, 16])
    d_normw = din("normw_col", [128, 8])
    d_sgug = din("sgu_g", [1, D])
    d_sgub = din("sgu_b", [1, D])
    d_ln1g = din("ln1_g", [1, D])
    d_ln1b = din("ln1_b", [1, D])
    d_ln2g = din("ln2_g", [1, D])
    d_ln2b = din("ln2_b", [1, D])
    d_wspT = din("wspT", [128, 1024])
    d_bsp = din("bsp", [1, 1024])
    d_ident = din("ident", [128, 128])
    d_mle = din("m_le", [128, 128])
    d_mge = din("m_ge", [128, 128])
    d_mgt = din("m_gt", [128, 128])
    d_mlt = din("m_lt", [128, 128])
    d_jidx = din("jidx", [64, 256])
    d_pidx = din("pidx", [64, 1])
    d_selcol = din("selcol", [64, 128])
    d_selcolh = din("selcol_h", [64, 8])
    d_selrow = din("selrow", [NT - 1, 64, 512])
    d_selrowh = din("selrow_h", [NT - 1, 64, 8])
    d_flags = din("flags", [128, NT * 16])
    d_swf = din("swf", [128, 4])

    o_yp = dout("yp", [512, D])
    o_ys = dout("ys", [1024, D])
    o_nsf = dout("nsf", [2, D, 128])
    o_nsb = dout("nsb", [2, D, 128])

    es = ExitStack()
    S = Sched(nc)
    A = Arena(nc, es, "arena", 204 * 1024)
    PSB = [es.enter_context(nc.psum_tensor("ps%d" % i, [128, 512], F32)) for i in range(8)]
    ps_ctr = [0]

    def PS():
        b = PSB[ps_ctr[0] % 8]
        ps_ctr[0] += 1
        return b

    def act(out, in_, func, bias=None, scale=None):
        kw = {}
        rd = [in_]
        if bias is not None:
            kw["bias"] = bias
            if not isinstance(bias, (int, float)):
                rd.append(bias)
        if scale is not None:
            kw["scale"] = scale
            if not isinstance(scale, (int, float)):
                rd.append(scale)
        aset = {AF.Exp: "exp", AF.Ln: "exp", AF.Identity: None, AF.Copy: None}.get(func, str(func))
        S.add("act", lambda e: e.activation(out=out, in_=in_, func=func, **kw), rd, [out], actset=aset)

    def tt(out, a, b, op, eng="dve"):
        cost = None
        if eng == "dve" and a.dtype == BF16 and b.dtype == BF16 and out.dtype == BF16:
            n = 1
            for step, cnt in list(out.ap)[1:]:
                n *= int(cnt)
            cost = n / 1920.0 + 0.157
        S.add(eng, lambda e: e.tensor_tensor(out=out, in0=a, in1=b, op=op), [a, b], [out], cost=cost)

    def ts(out, a, s1, op0, s2=None, op1=None, eng="dve"):
        rd = [a]
        if not isinstance(s1, (int, float)):
            rd.append(s1)
        if s2 is not None and not isinstance(s2, (int, float)):
            rd.append(s2)
        if op1 is None:
            S.add(eng, lambda e: e.tensor_scalar(out=out, in0=a, scalar1=s1, scalar2=None, op0=op0), rd, [out])
        else:
            S.add(eng, lambda e: e.tensor_scalar(out=out, in0=a, scalar1=s1, scalar2=s2, op0=op0, op1=op1), rd, [out])

    def stt(out, a, s, b, op0, op1):
        rd = [a, b]
        if not isinstance(s, (int, float)):
            rd.append(s)
        S.add("dve", lambda e: e.scalar_tensor_tensor(out=out, in0=a, scalar=s, in1=b, op0=op0, op1=op1), rd, [out])

    PEN = "pool"
    OFF = OFFLOAD

    def cp(out, in_, eng="dve"):
        if eng == "act":
            act(out, in_, AF.Identity)
        else:
            S.add(eng, lambda e: e.tensor_copy(out=out, in_=in_), [in_], [out])

    def memset(ap, v, eng="dve"):
        S.add(eng, lambda e: e.memset(ap, v), [], [ap])

    def mm(out, lhsT, rhs, start=True, stop=True):
        n = 1
        for step, cnt in list(rhs.ap)[1:]:
            n *= int(cnt)
        c = max(max(n, 16) / 2300.0 * (4.0 if lhsT.dtype == F32 else 1.0), 0.035) + (0.08 if lhsT.dtype == F32 else 0.0)
        S.add("pe", lambda e: e.matmul(out, lhsT=lhsT, rhs=rhs, start=start, stop=stop), [lhsT, rhs], [out], cost=c)

    def tr(out, in_, identity):
        c = 0.2 if in_.dtype == F32 else 0.1
        S.add("pe", lambda e: e.transpose(out=out, in_=in_, identity=identity), [in_, identity], [out], cost=c)

    ident = A.alloc([128], F32)
    identb = A.alloc([128], BF16)
    m_le = A.alloc([128], F32)
    m_ge = A.alloc([128], F32)
    m_gt = A.alloc([128], F32)
    m_lt = A.alloc([128], F32)
    m_gtb = A.alloc([128], BF16)
    m_leb = A.alloc([128], BF16)
    m_geb = A.alloc([128], BF16)
    m_ltb = A.alloc([128], BF16)
    onesf = A.alloc([128], F32)
    onesb = A.alloc([128], BF16)
    cst = A.alloc([8], F32)
    lnA = A.alloc([D], F32)
    lnB = A.alloc([D], F32)
    gates = A.alloc([2, 2, D], F32)
    modcol = A.alloc([4, 8, 2], F32)
    convw = A.alloc([12, 3], F32)
    convb = A.alloc([12], F32)
    dtb = A.alloc([32], F32)
    aneg = A.alloc([32], F32)
    dskip = A.alloc([16], F32)
    normw = A.alloc([8], F32)
    flags = A.alloc([NT * 16], F32)
    swf = A.alloc([4], F32)
    Epos = A.alloc([512], F32)
    Eposb = A.alloc([512], BF16)
    posc = A.alloc([512], F32)
    posch = A.alloc([512], F32)
    selcol = A.alloc([128], F32)
    selcolh = A.alloc([8], F32)
    selc = A.alloc([EXT], F32)
    PCT = A.alloc([4, EXT], F32)
    jidx = A.alloc([256], F32)
    pidx = A.alloc([2], F32)
    wspT = A.alloc([8, 128], BF16)
    bspf = A.alloc([1024], F32)
    bsph = A.alloc([1024], BF16)
    bspl = A.alloc([1024], BF16)
    Hf0 = A.alloc([D], F32)
    Hf = A.alloc([D], F32)
    Hb = A.alloc([D], F32)
    ring = [A.alloc([8, 512], BF16) for _ in range(NRING)]
    badacol = A.alloc([48], F32)
    ccol = A.alloc([8, 2], F32)
    sc2 = A.alloc([8, 2], BF16)
    scr = A.alloc([64], F32)
    TB = A.off
    hT = A.view(TB, [8, NCH, EXT], BF16)
    R = TB + 8 * NCH * EXT * 2
    KB = 1024

    def RV(off, shape, dt):
        return A.view(R + off, shape, dt)

    def rowb(src):
        n = int(src.shape[-1])
        return bass.AP(src.tensor, int(src.offset), [[0, 128], [1, n]])

    def cl(dst, src):
        S.dma("sp", dst, src, "const", chain=False)

    cl(ident, d_ident); cl(m_le, d_mle); cl(m_ge, d_mge); cl(m_gt, d_mgt); cl(m_lt, d_mlt)
    cl(convw.rearrange("p a b -> p (a b)"), d_convw); cl(convb, d_convb)
    cl(dtb, rowb(d_dtb)); cl(aneg, rowb(d_alog))
    cl(dskip, rowb(d_dskip)); cl(normw, d_normw)
    cl(flags, d_flags)
    cl(swf, d_swf)
    cl(selcol[0:64, :], d_selcol); cl(selcolh[0:64, :], d_selcolh); cl(selc[0:64, :], d_selc)
    cl(jidx[0:64, :], d_jidx); cl(pidx[0:64, 0:1], d_pidx)
    cl(bspf[0:1, :], d_bsp)
    cl(badacol, d_badacol); cl(ccol.rearrange("p a b -> p (a b)"), d_ccol)
    S.dma("pool", wspT.rearrange("p a b -> p (a b)"), d_wspT, "constp", chain=False)

    memset(cst[:, 0:1], -math.pi)
    memset(cst[:, 1:2], 1.0)
    memset(cst[:, 2:3], EPS)
    memset(onesf, 1.0)
    memset(onesb, 1.0)
    cp(identb, ident)
    cp(m_gtb, m_gt)
    cp(m_leb, m_le)
    cp(m_geb, m_ge)
    cp(m_ltb, m_lt)
    act(aneg, aneg, AF.Exp)
    ts(aneg, aneg, -1.0, ALU.mult)
    cp(bsph[0:1, :], bspf[0:1, :])
    tt(bspf[0:1, :], bspf[0:1, :], bsph[0:1, :], ALU.subtract)
    cp(bspl[0:1, :], bspf[0:1, :])

    ring_ctr = [0]

    def wtile(src, ncols):
        i = ring_ctr[0] % NRING
        ring_ctr[0] += 1
        dst = ring[i][:, :, 0:ncols]
        S.dma("pool", dst, src.rearrange("(k p) c -> p k c", p=128), "w%d" % i)
        return dst

    def build_pos():
        E = Epos[0:64, :]
        om = RV(0, [256], F32)[0:64, :]
        ang = RV(1 * KB, [256], F32)[0:64, :]
        r = RV(2 * KB, [256], F32)[0:64, :]
        kf = RV(3 * KB, [256], F32)[0:64, :]
        ki = RV(4 * KB, [256], F32)[0:64, :].bitcast(I32)
        m = RV(5 * KB, [256], F32)[0:64, :]
        y = RV(6 * KB, [256], F32)[0:64, :]
        act(om, jidx[0:64, :], AF.Exp, scale=-math.log(10000.0) / 256.0)
        ts(ang, om, pidx[0:64, 0:1], ALU.mult)
        ts(r, ang, 1.0 / (2 * math.pi), ALU.mult)
        cp(ki, r)
        cp(kf, ki)
        stt(y, kf, -2 * math.pi, ang, ALU.mult, ALU.add)
        for half, shift in ((0, 0.0), (1, math.pi / 2)):
            ts(r, y, shift, ALU.add)
            ts(m, r, math.pi, ALU.is_gt)
            stt(r, m, -2 * math.pi, r, ALU.mult, ALU.add)
            ts(r, r, 3.1415925, ALU.min, -3.1415925, ALU.max)
            act(E[:, half * 256:(half + 1) * 256], r, AF.Sin)
        p = PS()
        mm(p[:, :], selcol[0:64, :], E)
        cp(posc, p[:, :])
        p = PS()
        mm(p[0:8, :], selcolh[0:64, :], E)
        cp(posch[0:8, :], p[0:8, :])
        cp(Eposb[0:64, :], E)
        for h2 in range(2):
            p = PS()
            for q in range(2):
                fc = h2 * 2 + q
                mm(p[:, q * EXT:(q + 1) * EXT], E[:, fc * 128:(fc + 1) * 128], selc[0:64, :])
            cp(PCT[:, h2 * 2:h2 * 2 + 2, :].rearrange("p a b -> p (a b)"), p[:, 0:2 * EXT])

    def mod_init():
        act(ccol, ccol, AF.Silu)
        cp(sc2, ccol)

    def mod_tiles(tiles):
        kindmap = {0: 0, 1: 1, 3: 2, 4: 3}
        for t in tiles:
            c0 = t * 512
            kind = c0 // D
            half = (c0 % D) // 512
            wt = wtile(w_ada[:, c0:c0 + 512], 512)
            if kind in (2, 5):
                wi = 0 if kind == 2 else 1
                bb = lnA[:, half * 512:(half + 1) * 512]
                S.dma("sp", bb, rowb(d_badarow[:, c0:c0 + 512]), "misc")
                for cond in range(2):
                    p = PS()
                    for k in range(8):
                        mm(p[:, :], fpat(sc2[:, k, cond:cond + 1], [[0, 128]]), wt[:, k, :], k == 0, k == 7)
                    tt(gates[:, cond, wi, half * 512:(half + 1) * 512], p[:, :], bb, ALU.add)
            else:
                km = kindmap[kind]
                for s in range(4):
                    fc = half * 4 + s
                    p = PS()
                    for k in range(8):
                        mm(p[:, 0:2], wt[:, k, s * 128:(s + 1) * 128], sc2[:, k, :], k == 0, k == 7)
                    col = kind * 8 + fc
                    ts(modcol[:, km, fc, :], p[:, 0:2], badacol[:, col:col + 1], ALU.add,
                       1.0 if km in (1, 3) else 0.0, ALU.add)

    def load_x(ti, x_tm, xh_rows, with_halo):
        assert not with_halo
        S.dma("sp", x_tm, xres[{0: 0, 8: 1, 9: 2}[ti], :, :].rearrange("(c p) f -> p c f", p=128), "x")
        if ti >= 1:
            pi = ti - 1
            selrow = RV(84 * KB, [512], F32)[0:64, :]
            selrowh = RV(86 * KB, [8], F32)[0:64, :]
            S.dma("sp", selrow, d_selrow[pi], "misc")
            for c in range(NCH):
                p = PS()
                mm(p[:, :], selrow[:, c * 128:(c + 1) * 128], Epos[0:64, :])
                tt(x_tm[:, c, 0:512], x_tm[:, c, 0:512], p[:, :], ALU.add)
                tt(x_tm[:, c, 512:1024], x_tm[:, c, 512:1024], posc, ALU.add)
            if with_halo:
                S.dma("sp", selrowh, d_selrowh[pi], "misc")
                p = PS()
                mm(p[0:8, :], selrowh, Epos[0:64, :])
                tt(xh_rows[:, 0:512], xh_rows[:, 0:512], p[0:8, :], ALU.add)
                tt(xh_rows[:, 512:1024], xh_rows[:, 512:1024], posch[0:8, :], ALU.add)

    def make_hT(x_tm, xh_rows, cond, kshift, kscale, ti, with_halo):
        for f in range(8):
            p = PS()
            for c in range(NCH):
                tr(p[:, c * 128:(c + 1) * 128], x_tm[:, c, f * 128:(f + 1) * 128], ident)
            act(hT[:, f, :, 1:129], p[:, :].rearrange("p (c t) -> p c t", c=NCH), AF.Identity,
                bias=modcol[:, kshift, f, cond:cond + 1], scale=modcol[:, kscale, f, cond:cond + 1])
        if with_halo:
            p = PS()
            for f in range(8):
                tr(p[:, f * 8:(f + 1) * 8], xh_rows[:, f * 128:(f + 1) * 128], ident[0:8, 0:8])
            for f in range(8):
                outv = fpat(hT[:, f, 0, 0:1], [[EXT, NCH], [EXT - 1, 2]])
                inv = p[:, f * 8:(f + 1) * 8].rearrange("p (c s) -> p c s", c=NCH)
                act(outv, inv, AF.Identity, bias=modcol[:, kshift, f, cond:cond + 1],
                    scale=modcol[:, kscale, f, cond:cond + 1])
            hv = fpat(hT[:, 0, 0, 0:1], [[NCH * EXT, 8], [EXT, NCH], [EXT - 1, 2]])
            fv = fpat(flags[:, ti * 16:ti * 16 + 1], [[0, 8], [4, NCH], [1, 2]])
            tt(hv, hv, fv, ALU.mult)

    HTC = [hT]

    def make_hT_fm(ti, cond, xT_st, selT_t):
        hTc = HTC[0]
        S.dma("sp", xT_st, xTe[ti].rearrange("(k p) t -> p k t", p=128), "x")
        if ti >= 1:
            S.dma("pool", selT_t, d_selT[ti - 1], "selt")
            for fc in range(4):
                for half in range(2):
                    p = PS()
                    mm(p[:, 0:2 * EXT], Eposb[0:64, fc * 128:(fc + 1) * 128], selT_t[:, half * 2 * EXT:(half + 1) * 2 * EXT])
                    xs = xT_st[:, fc, half * 2 * EXT:(half + 1) * 2 * EXT]
                    tt(xs, xs, p[:, 0:2 * EXT], ALU.add)
            xc = xT_st[:, 4:8, :].rearrange("p f (c e) -> p f c e", c=NCH)
            tt(xc, xc, fpat(PCT[:, 0, 0:1], [[EXT, 4], [0, NCH], [1, EXT]]), ALU.add, eng=PEN if "posP" in OFF else "dve")
        for f in range(8):
            act(hTc[:, f, :, :].rearrange("p c e -> p (c e)"), xT_st[:, f, :], AF.Identity,
                bias=modcol[:, 0, f, cond:cond + 1], scale=modcol[:, 1, f, cond:cond + 1])
        hv = fpat(hTc[:, 0, 0, 0:1], [[NCH * EXT, 8], [EXT, NCH], [EXT - 1, 2]])
        fv = fpat(flags[:, ti * 16:ti * 16 + 1], [[0, 8], [4, NCH], [1, 2]])
        tt(hv, hv, fv, ALU.mult)

    hT_main = lambda k: HTC[0][:, k, :, 1:129]
    hT_chunk = lambda k, c: HTC[0][:, k, c, 1:129]

    XO = 42 * KB
    xh_tm = RV(0, [NCH, D], F32)
    BCT = RV(16 * KB, [4, 512], BF16)
    Bt = RV(20 * KB, [NCH, 2, 128], BF16)
    Esb = RV(22 * KB, [NCH, 96], F32)
    dtt = RV(22 * KB + 1536, [NCH, 32], F32)
    att = RV(24 * KB, [NCH, 32], F32)
    sm = RV(24 * KB + 512, [256], F32)
    zs = RV(26 * KB, [NCH, D], BF16)
    y_a = RV(34 * KB, [NCH, D], BF16)
    xcT = RV(XO, [12, 512], F32)
    S_b = RV(XO, [NCH, D], F32)
    Hs = RV(XO + 18 * KB, [NCH, 2, D], BF16)
    xdA = RV(XO + 34 * KB, [D], BF16)
    xdB = RV(XO + 36 * KB, [D], BF16)
    Msel = RV(XO + 38 * KB, [128], F32)
    rhs_seg = RV(XO, [8, 128], BF16)
    exps = RV(XO + 4 * KB, [8, 128], BF16)
    cbm = RV(XO + 8 * KB, [2, 2, 128], BF16)
    MT = RV(XO + 10 * KB, [2, 16, 128], BF16)
    t1 = RV(XO + 38 * KB + 512, [512], F32)
    t2 = RV(XO + 40 * KB + 512, [512], F32)
    yc = RV(XO + 42 * KB + 512, [D], F32)
    assert XO + 46 * KB + 512 <= 89 * KB
    assert R + 93 * KB <= A.nbytes, (R, A.nbytes)

    class BufSet:
        pass

    BS0 = BufSet()
    BS0.xh_tm, BS0.BCT, BS0.Bt, BS0.dtt, BS0.att, BS0.xcT = xh_tm, BCT, Bt, dtt, att, xcT
    attb = RV(25 * KB + 512, [NCH, 32], BF16)
    BS0.attb = attb

    def ssd_front(ti, bs=None, parts="abc", need_c=True):
        bs = bs or BS0
        xcT_, BCT_ = bs.xcT, bs.BCT
        if "a" in parts:
            for wi in range(3):
                ncol = 512 if (need_c or wi < 2) else 256
                wt = wtile(w_in[:, 1024 + wi * 512:1024 + wi * 512 + ncol], ncol)
                for s in range(ncol // 128):
                    j = wi * 4 + s
                    pp = [PS(), PS()]
                    for half in range(2):
                        for k in range(8):
                            rhs = fpat(HTC[0][:, k, 2 * half, 0:1], [[1, 2 * EXT]])
                            mm(pp[half][:, 0:2 * EXT], wt[:, k, s * 128:(s + 1) * 128], rhs, k == 0, k == 7)
                    for half in range(2):
                        pv = pp[half][:, 0:2 * EXT].rearrange("p (c t) -> p c t", c=2)
                        acc = xcT_[:, j, half * 256:(half + 1) * 256].rearrange("p (c t) -> p c t", c=2)
                        act(acc, pv[:, :, 1:129], AF.Identity, bias=convb[:, j:j + 1], scale=convw[:, j, 1:2])
                        stt(acc, pv[:, :, 0:128], convw[:, j, 0:1], acc, ALU.mult, ALU.add)
                        stt(acc, pv[:, :, 2:130], convw[:, j, 2:3], acc, ALU.mult, ALU.add)
                    if j < 8:
                        if getattr(bs, "xsb", None) is not None:
                            act(bs.xsb[:, j, :], xcT_[:, j, :], AF.Silu)
                        else:
                            act(xcT_[:, j, :], xcT_[:, j, :], AF.Silu)
                    else:
                        act(BCT_[:, j - 8, :], xcT_[:, j, :], AF.Silu)
        if "b" in parts:
            wt = wtile(w_in[:, 2560:2592], 32)
            p = PS()
            for c in range(NCH):
                for k in range(8):
                    mm(p[:, c * 32:(c + 1) * 32], hT_chunk(k, c), wt[:, k, 0:32], k == 0, k == 7)
            tt(bs.dtt, p[:, 0:128].rearrange("p (c h) -> p c h", c=NCH), fpat(dtb, [[0, NCH], [1, 32]]), ALU.add)
            act(bs.dtt, bs.dtt, AF.Exp)
            act(bs.dtt, bs.dtt, AF.Ln, bias=cst[:, 1:2], scale=1.0)
            tt(bs.att, bs.dtt, fpat(aneg, [[0, NCH], [1, 32]]), ALU.mult)
            if getattr(bs, "attb", None) is not None:
                cp(bs.attb, bs.att)
        if "c" in parts:
            for c in range(NCH if getattr(bs, "xsb", None) is not None else 0):
                p = PS()
                pb = p.bitcast(BF16)
                for f in range(8):
                    tr(pb[:, f * 128:(f + 1) * 128], bs.xsb[:, f, c * 128:(c + 1) * 128], identb)
                cp(bs.xh_tm[:, c, :], pb[:, 0:1024], eng="act" if c % 2 else "dve")
            for c in range(0 if getattr(bs, "xsb", None) is not None else NCH):
                for hh in range(2):
                    p = PS()
                    for q in range(4):
                        f = hh * 4 + q
                        tr(p[:, q * 128:(q + 1) * 128], xcT_[:, f, c * 128:(c + 1) * 128], ident)
                    cp(bs.xh_tm[:, c, hh * 512:(hh + 1) * 512], p[:, :], eng="act" if hh else "dve")
            p = PS()
            pb = p.bitcast(BF16)
            for c in range(NCH):
                for g in range(2):
                    i = c * 2 + g
                    tr(pb[:, i * 128:(i + 1) * 128], BCT_[:, g, c * 128:(c + 1) * 128], identb)
            cp(bs.Bt.rearrange("p c g n -> p (c g n)"), pb[:, 0:1024])

    def bc_heads(ap16, nh=16):
        return fpat(ap16, [[1, nh], [0, 64]])

    def v3(ap, nh=16):
        return ap.rearrange("p (h d) -> p h d", h=nh)

    def transposeH_out(Hacc, dst):
        for hh in range(2):
            p = PS()
            for q in range(4):
                f = hh * 4 + q
                tr(p[:, q * 128:(q + 1) * 128], Hacc[:, f * 128:(f + 1) * 128], ident)
            tmp = t1 if hh == 0 else t2
            cp(tmp, p[:, :], eng="act")
            S.dma("sp", dst[hh * 512:(hh + 1) * 512, :].rearrange("(q p) n -> p q n", p=128),
                  tmp.rearrange("p (q n) -> p q n", q=4), "st")

    def loadH(Hacc, src):
        for hh in range(2):
            tmp = t1 if hh == 0 else t2
            S.dma("sp", tmp.rearrange("p (q n) -> p q n", q=4),
                  src[hh * 512:(hh + 1) * 512, :].rearrange("(q p) n -> p q n", p=128), "misc")
            p = PS()
            for q in range(4):
                tr(p[:, q * 128:(q + 1) * 128], tmp[:, q * 128:(q + 1) * 128], ident)
            cp(Hacc[:, hh * 512:(hh + 1) * 512], p[:, :], eng="act")

    def ssd_phaseAB(fchain, bchain):
        finfo = {c: (Hacc, rs, fd) for (c, Hacc, rs, fd) in fchain}
        for c in range(NCH):
            p = PS()
            mm(p[:, 0:16], m_leb, attb[:, c, 0:16])
            mm(p[:, 16:32], m_geb, attb[:, c, 16:32])
            mm(p[:, 32:48], m_gtb, attb[:, c, 0:16])
            mm(p[:, 48:64], m_ltb, attb[:, c, 16:32])
            mm(p[:, 64:96], onesb, attb[:, c, 0:32])
            act(Esb[:, c, :], p[:, 0:96], AF.Exp)
            wd = sm[:, 0:32]
            tt(wd, dtt[:, c, :], Esb[:, c, 32:64], ALU.mult)
            tt(v3(xdA), v3(xh_tm[:, c, :]), bc_heads(wd[:, 0:16]), ALU.mult, eng=PEN if "xdA" in OFF else "dve")
            tt(v3(xdB), v3(xh_tm[:, c, :]), bc_heads(wd[:, 16:32]), ALU.mult, eng=PEN if "xdA" in OFF else "dve")
            Hacc, rs, fd = finfo[c]
            if rs:
                memset(Hacc, 0.0)
            cp(Hs[:, c, 0, :], Hacc, eng="act")
            tt(v3(Hacc), v3(Hacc), bc_heads(Esb[:, c, 64:80]), ALU.mult, eng=PEN if "HA" in OFF else "dve")
            for g in range(2):
                p = PS()
                mm(p[:, :], Bt[:, c, g, :], xdA[:, g * 512:(g + 1) * 512])
                tt(Hacc[:, g * 512:(g + 1) * 512], Hacc[:, g * 512:(g + 1) * 512], p[:, :], ALU.add)
            for g in range(2):
                p = PS()
                mm(p[:, :], Bt[:, c, g, :], xdB[:, g * 512:(g + 1) * 512])
                cp(S_b[:, c, g * 512:(g + 1) * 512], p[:, :], eng="act")
            if fd is not None:
                transposeH_out(Hacc, fd)
        for (c, Hacc, rs, fd) in bchain:
            if rs:
                memset(Hacc, 0.0)
            cp(Hs[:, c, 1, :], Hacc, eng="act")
            tt(v3(Hacc), v3(Hacc), bc_heads(Esb[:, c, 80:96]), ALU.mult, eng=PEN if "HB" in OFF else "dve")
            tt(Hacc, Hacc, S_b[:, c, :], ALU.add, eng=PEN if "HB" in OFF else "dve")
            if fd is not None:
                transposeH_out(Hacc, fd)

    def z_proj():
        for hw in range(2):
            wt = wtile(w_in[:, hw * 512:(hw + 1) * 512], 512)
            for c in range(NCH):
                p = PS()
                for k in range(8):
                    mm(p[:, :], hT_chunk(k, c), wt[:, k, :], k == 0, k == 7)
                act(zs[:, c, hw * 512:(hw + 1) * 512], p[:, :], AF.Silu)

    def ssd_phaseC():
        t3s = [RV(XO + 47 * KB, [512], F32), RV(XO + 49 * KB, [512], F32)]
        for c in range(NCH):
            tok = slice(c * 128, (c + 1) * 128)
            p = PS()
            for g in range(2):
                mm(p[:, g * 128:(g + 1) * 128], BCT[:, g, tok], BCT[:, 2 + g, tok])
            pv = p[:, 0:256].rearrange("p (g i) -> p g i", g=2)
            tt(cbm[:, 0, :, :], pv, fpat(m_le, [[0, 2], [1, 128]]), ALU.mult)
            tt(cbm[:, 1, :, :], pv, fpat(m_ge, [[0, 2], [1, 128]]), ALU.mult)
            tt(v3(xdA), v3(xh_tm[:, c, :]), bc_heads(dtt[:, c, 0:16]), ALU.mult, eng=PEN if "xdC" in OFF else "dve")
            tt(v3(xdB), v3(xh_tm[:, c, :]), bc_heads(dtt[:, c, 16:32]), ALU.mult, eng=PEN if "xdC" in OFF else "dve")
            for d in range(2):
                msk = m_le if d == 0 else m_ge
                lmask = m_gtb if d == 0 else m_ltb
                for hh in range(2):
                    for hl in range(8):
                        col = d * 16 + hh * 8 + hl
                        act(rhs_seg[:, hl, :], msk, AF.Identity, scale=att[:, c, col:col + 1])
                    for q in range(2):
                        p = PS()
                        mm(p[:, :], lmask, rhs_seg[:, q * 4:(q + 1) * 4, :].rearrange("p h i -> p (h i)"))
                        act(exps[:, q * 4:(q + 1) * 4, :].rearrange("p h i -> p (h i)"), p[:, :], AF.Exp)
                    tt(MT[:, d, hh * 8:(hh + 1) * 8, :], exps, fpat(cbm[:, d, hh, :], [[0, 8], [1, 128]]), ALU.mult)
            for g in range(2):
                pd = PS()
                for hl in range(8):
                    h = g * 8 + hl
                    mm(pd[:, hl * 64:(hl + 1) * 64], MT[:, 0, h, :], xdA[:, h * 64:(h + 1) * 64], True, False)
                    mm(pd[:, hl * 64:(hl + 1) * 64], MT[:, 1, h, :], xdB[:, h * 64:(h + 1) * 64], False, True)
                pf = PS()
                mm(pf[:, :], BCT[:, 2 + g, tok], Hs[:, c, 0, g * 512:(g + 1) * 512])
                pbk = PS()
                mm(pbk[:, :], BCT[:, 2 + g, tok], Hs[:, c, 1, g * 512:(g + 1) * 512])
                ycg = yc[:, g * 512:(g + 1) * 512]
                tt(v3(t1, 8), v3(pf[:, :], 8), bc_heads(Esb[:, c, g * 8:g * 8 + 8], 8), ALU.mult)
                tt(v3(t2, 8), v3(pbk[:, :], 8), bc_heads(Esb[:, c, 16 + g * 8:16 + g * 8 + 8], 8), ALU.mult)
                tt(ycg, pd[:, :], t1, ALU.add)
                tt(ycg, ycg, t2, ALU.add, eng=PEN)
                t3 = t3s[g]
                tt(v3(t3, 8), v3(xh_tm[:, c, g * 512:(g + 1) * 512], 8), bc_heads(dskip[:, g * 8:g * 8 + 8], 8), ALU.mult, eng=PEN)
                tt(ycg, ycg, t3, ALU.add, eng=PEN)
            tt(yc, yc, zs[:, c, :], ALU.mult, eng=PEN if "gate" in OFF else "dve")
            st = scr[:, 0:12]
            mv = scr[:, 12:14]
            S.add("dve", lambda e, yc=yc, st=st: e.bn_stats(out=st[:, 0:6], in_=yc[:, 0:512]), [yc], [st[:, 0:6]])
            S.add("dve", lambda e, yc=yc, st=st: e.bn_stats(out=st[:, 6:12], in_=yc[:, 512:1024]), [yc], [st[:, 6:12]])
            S.add("dve", lambda e, st=st, mv=mv: e.bn_aggr(out=mv, in_=st), [st], [mv])
            r = scr[:, 14:15]
            tt(r, mv[:, 0:1], mv[:, 0:1], ALU.mult)
            tt(r, r, mv[:, 1:2], ALU.add)
            act(r, r, AF.Sqrt, bias=cst[:, 2:3], scale=1.0)
            S.add("dve", lambda e, r=r: e.reciprocal(out=r, in_=r), [r], [r])
            act(y_a[:, c, :], yc, AF.Identity, scale=r)

    def make_yaT(y_aT):
        base = ps_ctr[0]
        ps_ctr[0] += 2
        for f in range(8):
            p = PSB[(base + (f % 2)) % 8]
            pb = p.bitcast(BF16)
            for c in range(NCH):
                tr(pb[:, c * 128:(c + 1) * 128], y_a[:, c, f * 128:(f + 1) * 128], identb)
            act(y_aT[:, f, :], pb[:, 0:512], AF.Identity, scale=normw[:, f:f + 1])

    def layer_norm(x, out, g_bc, b_bc):
        st = scr[:, 16:28]
        mv = scr[:, 28:30]
        S.add("dve", lambda e: e.bn_stats(out=st[:, 0:6], in_=x[:, 0:512]), [x], [st[:, 0:6]])
        S.add("dve", lambda e: e.bn_stats(out=st[:, 6:12], in_=x[:, 512:1024]), [x], [st[:, 6:12]])
        S.add("dve", lambda e: e.bn_aggr(out=mv, in_=st), [st], [mv])
        r = scr[:, 30:31]
        act(r, mv[:, 1:2], AF.Sqrt, bias=cst[:, 2:3], scale=1.0)
        S.add("dve", lambda e: e.reciprocal(out=r, in_=r), [r], [r])
        stt(x, x, mv[:, 0:1], g_bc, ALU.subtract, ALU.mult)
        stt(out, x, r, b_bc, ALU.mult, ALU.add)

    def load_ln(gsrc, bsrc):
        S.dma("sp", lnA, rowb(gsrc), "lnp")
        S.dma("sp", lnB, rowb(bsrc), "lnp")

    def sgu(y_bT):
        guT = RV(8 * KB, [8, 512], F32)
        vtmp = RV(24 * KB, [NCH, D], F32)
        v_ln = RV(40 * KB, [NCH, D], BF16)
        for wi in range(2):
            wt = wtile(w_in[:, 2592 + wi * 512:2592 + (wi + 1) * 512], 512)
            for s in range(4):
                fc = wi * 4 + s
                p = PS()
                for k in range(8):
                    mm(p[:, :].rearrange("p (c t) -> p c t", c=NCH), wt[:, k, s * 128:(s + 1) * 128], hT_main(k), k == 0, k == 7)
                act(guT[:, fc, :], p[:, :], AF.Gelu)
        for hw in range(2):
            wt = wtile(w_in[:, 3616 + hw * 512:3616 + (hw + 1) * 512], 512)
            for c in range(NCH):
                p = PS()
                for k in range(8):
                    mm(p[:, :], hT_chunk(k, c), wt[:, k, :], k == 0, k == 7)
                act(vtmp[:, c, hw * 512:(hw + 1) * 512], p[:, :], AF.Gelu)
        load_ln(d_sgug, d_sgub)
        for c in range(NCH):
            layer_norm(vtmp[:, c, :], v_ln[:, c, :], lnA, lnB)
        for c in range(NCH):
            for q in range(2):
                p = PS()
                for gl in range(4):
                    g = q * 4 + gl
                    o = p[:, gl * 128:(gl + 1) * 128]
                    mm(o, v_ln[:, c, g * 128:(g + 1) * 128], wspT[:, g, :], True, False)
                    mm(o, onesb[0:1, :], bsph[0:1, g * 128:(g + 1) * 128], False, False)
                    mm(o, onesb[0:1, :], bspl[0:1, g * 128:(g + 1) * 128], False, True)
                tt(y_bT[:, q * 4:(q + 1) * 4, c * 128:(c + 1) * 128],
                   p[:, :].rearrange("p (g i) -> p g i", g=4),
                   guT[:, q * 4:(q + 1) * 4, c * 128:(c + 1) * 128], ALU.mult)

    def merge(y_aT, y_bT, mergedT):
        mA = RV(8 * KB, [8, 512], F32)
        sg = RV(24 * KB, [512], F32)
        tm = RV(26 * KB, [512], F32)
        for br in range(2):
            src = y_aT if br == 0 else y_bT
            wproj = w_pa if br == 0 else w_pb
            gbase = 4640 + br * 1024
            for hw in range(2):
                wg = wtile(w_in[:, gbase + hw * 512:gbase + (hw + 1) * 512], 512)
                wp = wtile(wproj[:, hw * 512:(hw + 1) * 512], 512)
                for s in range(4):
                    fc = hw * 4 + s
                    pg = PS()
                    for k in range(8):
                        mm(pg[:, :].rearrange("p (c t) -> p c t", c=NCH), wg[:, k, s * 128:(s + 1) * 128], hT_main(k), k == 0, k == 7)
                    pp = PS()
                    for k in range(8):
                        mm(pp[:, :], wp[:, k, s * 128:(s + 1) * 128], src[:, k, :], k == 0, k == 7)
                    act(sg, pg[:, :], AF.Sigmoid)
                    if br == 0:
                        tt(mA[:, fc, :], pp[:, :], sg, ALU.mult)
                    else:
                        tt(tm, pp[:, :], sg, ALU.mult)
                        tt(mergedT[:, fc, :], tm, mA[:, fc, :], ALU.add)

    def resid_ln(x_tm, c, pbanks, gate_bc, out_ap):
        for hw in range(2):
            tm = RV(24 * KB + hw * 2 * KB, [512], F32)
            tt(tm, pbanks[hw][:, :], gate_bc[:, hw * 512:(hw + 1) * 512], ALU.mult)
            xs = x_tm[:, c, hw * 512:(hw + 1) * 512]
            stt(xs, xs, ALPHA, tm, ALU.mult, ALU.add)
        layer_norm(x_tm[:, c, :], out_ap, lnA, lnB)

    def out_proj(x_tm, mergedT, cond):
        load_ln(d_ln1g, d_ln1b)
        wts = [wtile(w_out[:, hw * 512:(hw + 1) * 512], 512) for hw in range(2)]
        for c in range(NCH):
            pb = []
            for hw in range(2):
                p = PS()
                for k in range(8):
                    mm(p[:, :], mergedT[:, k, c * 128:(c + 1) * 128], wts[hw][:, k, :], k == 0, k == 7)
                pb.append(p)
            resid_ln(x_tm, c, pb, gates[:, cond, 0, :], x_tm[:, c, :])

    def ffn(x_tm, cond, out_dram):
        f1T = RV(28 * KB, [32, 512], BF16)
        rl = [RV(16 * KB, [512], F32), RV(18 * KB, [512], F32)]
        for wi in range(8):
            wt = wtile(w_ff1[:, wi * 512:(wi + 1) * 512], 512)
            for s in range(4):
                fc = wi * 4 + s
                p = PS()
                for k in range(8):
                    mm(p[:, :].rearrange("p (c t) -> p c t", c=NCH), wt[:, k, s * 128:(s + 1) * 128], hT_main(k), k == 0, k == 7)
                r = rl[fc % 2]
                act(r, p[:, :], AF.Relu)
                tt(f1T[:, fc, :], r, r, ALU.mult)
        load_ln(d_ln2g, d_ln2b)
        pbs = [[None, None] for _ in range(NCH)]
        for hw in range(2):
            banks = [PS() for _ in range(NCH)]
            for kg in range(4):
                wt = wtile(w_ff2[kg * 1024:(kg + 1) * 1024, hw * 512:(hw + 1) * 512], 512)
                for c in range(NCH):
                    for k in range(8):
                        mm(banks[c][:, :], f1T[:, kg * 8 + k, c * 128:(c + 1) * 128], wt[:, k, :],
                           kg == 0 and k == 0, kg == 3 and k == 7)
            for c in range(NCH):
                tm = RV(24 * KB + hw * 2 * KB, [512], F32)
                tt(tm, banks[c][:, :], gates[:, cond, 1, hw * 512:(hw + 1) * 512], ALU.mult)
                xs = x_tm[:, c, hw * 512:(hw + 1) * 512]
                stt(xs, xs, ALPHA, tm, ALU.mult, ALU.add)
        for c in range(NCH):
            layer_norm(x_tm[:, c, :], x_tm[:, c, :], lnA, lnB)
            S.dma("sp", out_dram[c * 128:(c + 1) * 128, :], x_tm[:, c, :], "st")

    def full_tile(ti, cond, fchain, bchain, out_dram):
        x_tm = RV(64 * KB, [NCH, D], F32)
        xh_rows = None
        chk("t%d:x" % ti)
        make_hT_fm(ti, cond, RV(0, [8, NCH * EXT], F32), RV(17 * KB, [NCH * EXT], BF16)[0:64, :])
        chk("t%d:hT" % ti)
        ssd_front(ti)
        chk("t%d:front" % ti)
        ssd_phaseAB(fchain, bchain)
        chk("t%d:AB" % ti)
        z_proj()
        chk("t%d:z" % ti)
        ssd_phaseC()
        chk("t%d:C" % ti)
        y_aT = RV(0, [8, 512], BF16)
        make_yaT(y_aT)
        chk("t%d:yaT" % ti)
        y_bT = RV(48 * KB, [8, 512], BF16)
        sgu(y_bT)
        chk("t%d:sgu" % ti)
        mergedT = RV(56 * KB, [8, 512], BF16)
        merge(y_aT, y_bT, mergedT)
        chk("t%d:merge" % ti)
        load_x(ti, x_tm, xh_rows, False)
        if ti == 0:
            mod_tiles([4, 5])
        out_proj(x_tm, mergedT, cond)
        chk("t%d:outp" % ti)
        if ti == 0:
            mod_tiles([6, 7, 8, 9])
        make_hT(x_tm, None, cond, 2, 3, ti, False)
        chk("t%d:h2T" % ti)
        if ti == 0:
            mod_tiles([10, 11])
        ffn(x_tm, cond, out_dram)
        chk("t%d:ffn" % ti)

    PX = BufSet()
    PX.xT_st = RV(0, [8, NCH * EXT], F32)
    PX.selT = RV(16 * KB + 256, [NCH * EXT], BF16)[0:64, :]
    PX.x_tm = RV(0, [NCH, D], F32)
    PX.sm = RV(18 * KB + 512, [448], F32)
    PX.xcT = RV(20 * KB + 512, [10, 512], F32)
    PX.BCT = RV(40 * KB + 512, [2, 512], BF16)
    PX.xdA = RV(42 * KB + 512, [D], BF16)
    PX.Msel = RV(44 * KB + 512, [128], BF16)
    PX.aselb = RV(20 * KB + 256, [NCH, 16], BF16)
    hT2 = RV(45 * KB, [8, NCH, EXT], BF16)
    PX.xsb = RV(61 * KB + 512, [8, 512], BF16)
    PSET = []
    for i in range(2):
        b = BufSet()
        o = 53 * KB + 512 + i * 19 * KB
        b.xh_tm = RV(o, [NCH, D], BF16)
        b.xsb = PX.xsb
        b.Bt = RV(o + 16 * KB, [NCH, 2, 128], BF16)
        b.dtt = RV(o + 18 * KB, [NCH, 32], F32)
        b.att = RV(o + 18 * KB + 512, [NCH, 32], F32)
        b.xcT, b.BCT = PX.xcT, PX.BCT
        b.hT = hT if i == 0 else hT2
        PSET.append(b)
    PX.xdB2 = RV(91 * KB + 512, [D], BF16)
    assert R + 94 * KB <= A.nbytes, (R, A.nbytes)

    def prefix_front(ti, part):
        bs = PSET[ti % 2]
        HTC[0] = bs.hT
        if part == 1:
            make_hT_fm(ti, 1, PX.xT_st, PX.selT)
        elif part == 2:
            ssd_front(ti, bs, "a", need_c=False)
        else:
            ssd_front(ti, bs, "bc")
        HTC[0] = hT

    def prefix_slots(ti, chunks, Hc):
        if chunks[0] != 0:
            return
        bs = PSET[ti % 2]
        sm = PX.sm
        Msel_ = PX.Msel
        sf = flags[:, ti * 16 + 2:ti * 16 + 3]
        sb = flags[:, ti * 16 + 3:ti * 16 + 4]
        asel = PX.aselb
        dsel = sm[:, 64:128].rearrange("p (c h) -> p c h", c=NCH)
        tmp = sm[:, 128:192].rearrange("p (c h) -> p c h", c=NCH)
        ts(tmp, bs.att[:, :, 16:32], sb, ALU.mult)
        stt(asel, bs.att[:, :, 0:16], sf, tmp, ALU.mult, ALU.add)
        ts(tmp, bs.dtt[:, :, 16:32], sb, ALU.mult)
        stt(dsel, bs.dtt[:, :, 0:16], sf, tmp, ALU.mult, ALU.add)
        ts(Msel_, m_lt, sb, ALU.mult)
        stt(Msel_, m_gt, sf, Msel_, ALU.mult, ALU.add)
        p = PS()
        for c in range(NCH):
            mm(p[:, c * 32:c * 32 + 16], Msel_, asel[:, c, :])
            mm(p[:, c * 32 + 16:c * 32 + 32], onesb, asel[:, c, :])
        ex = sm[:, 192:320].rearrange("p (c h) -> p c h", c=NCH)
        act(ex, p[:, 0:128].rearrange("p (c h) -> p c h", c=NCH), AF.Exp)
        wd = sm[:, 320:384].rearrange("p (c h) -> p c h", c=NCH)
        tt(wd, dsel, ex[:, :, 0:16], ALU.mult)
        pc = sm[:, 384:400]
        cp(pc, ex[:, 3, 16:32])
        tt(wd[:, 2, :], wd[:, 2, :], pc, ALU.mult)
        tt(pc, pc, ex[:, 2, 16:32], ALU.mult)
        tt(wd[:, 1, :], wd[:, 1, :], pc, ALU.mult)
        tt(pc, pc, ex[:, 1, 16:32], ALU.mult)
        tt(wd[:, 0, :], wd[:, 0, :], pc, ALU.mult)
        tt(pc, pc, ex[:, 0, 16:32], ALU.mult)
        banks = [PS(), PS()]
        xds = [PX.xdA, PX.xdB2]
        for c in range(NCH):
            xd = xds[c % 2]
            tt(v3(xd), v3(bs.xh_tm[:, c, :]), bc_heads(wd[:, c, :]), ALU.mult, eng=PEN if (("xdP" in OFF and c % 2 == 1) or "xdPall" in OFF) else "dve")
            for g in range(2):
                mm(banks[g][:, :], bs.Bt[:, c, g, :], xd[:, g * 512:(g + 1) * 512], c == 0, c == NCH - 1)
        tt(v3(Hc), v3(Hc), bc_heads(pc), ALU.mult, eng=PEN if "HcP" in OFF else "dve")
        for g in range(2):
            gs = slice(g * 512, (g + 1) * 512)
            tt(Hc[:, gs], Hc[:, gs], banks[g][:, :], ALU.add)

    def switch_point(qp, Hc, Hinit, Hsave):
        w = swf[:, qp:qp + 1]
        tmp = PX.x_tm[:, 0, :]
        stt(Hsave, Hc, w, Hsave, ALU.mult, ALU.add)
        tt(tmp, Hinit, Hc, ALU.subtract)
        stt(Hc, tmp, w, Hc, ALU.mult, ALU.add)

    def prefix_all():
        memset(Hf0, 0.0)
        for part in (1, 2, 3):
            prefix_front(1, part)
        for ti in range(1, 8):
            nxt = ti + 1 if ti < 7 else None
            if ti in (1, 3, 5):
                switch_point((ti - 1) // 2, Hf, Hb, Hf0)
            if ti == 7:
                switch_point(3, Hf, Hb, Hf0)
                cp(Hb, Hf0)
            Hc = Hb if ti == 7 else Hf
            if nxt:
                prefix_front(nxt, 1)
            prefix_slots(ti, (0, 1), Hc)
            if nxt:
                prefix_front(nxt, 2)
            prefix_slots(ti, (2, 3), Hc)
            if nxt:
                prefix_front(nxt, 3)
            chk("t%d" % ti)

    DUMPABLE = {
        "modcol": (modcol.rearrange("p a b c -> p (a b c)"), 64),
        "gates": (gates.rearrange("p a b c -> p (a b c)"), 4 * D),
        "hT": (hT.rearrange("p a b c -> p (a b c)"), 8 * NCH * EXT),
        "xh_tm": (xh_tm.rearrange("p a b -> p (a b)"), NCH * D),
        "BCT": (BCT.rearrange("p a b -> p (a b)"), 4 * 512),
        "dtt": (dtt.rearrange("p a b -> p (a b)"), NCH * 32),
        "Esb": (Esb.rearrange("p a b -> p (a b)"), NCH * 96),
        "Hs": (Hs.rearrange("p a b c -> p (a b c)"), NCH * 2 * D),
        "zs": (zs.rearrange("p a b -> p (a b)"), NCH * D),
        "y_a": (y_a.rearrange("p a b -> p (a b)"), NCH * D),
        "y_aT": (RV(0, [8 * 512], BF16), 8 * 512),
        "y_bT": (RV(48 * KB, [8 * 512], BF16), 8 * 512),
        "mergedT": (RV(56 * KB, [8 * 512], BF16), 8 * 512),
        "x_tm": (RV(0, [NCH * D], F32), NCH * D),
        "Hf": (Hf, D), "Hb": (Hb, D), "Hf0": (Hf0, D),
        "Epos": (Epos[0:64, :], 512),
        "cbm": (cbm.rearrange("p a b c -> p (a b c)"), 512),
        "MT": (MT.rearrange("p a b c -> p (a b c)"), 2 * 16 * 128),
        "xdA": (xdA, D), "xdB": (xdB, D), "yc": (yc, D),
        "exps": (exps.rearrange("p a b -> p (a b)"), 1024),
        "rhs_seg": (rhs_seg.rearrange("p a b -> p (a b)"), 1024),
        "att": (att.rearrange("p a b -> p (a b)"), NCH * 32),
        "scr": (scr, 64),
    }

    def dump(name, ap, nfree):
        d = nc.dram_tensor("dbg_" + name, [int(list(ap.ap)[0][1]), nfree], ap.dtype, kind="ExternalOutput").ap()
        S.dma("sp", d, ap, "st")

    try:
        mod_init()
        mod_tiles([0, 1, 2, 3])
        chk("mod")
        full_tile(0, 0,
                  fchain=[(0, Hf, True, None), (1, Hf, False, o_nsf[0]), (2, Hf, True, None), (3, Hf, False, o_nsf[1])],
                  bchain=[(3, Hb, True, None), (2, Hb, False, o_nsb[1]), (1, Hb, True, None), (0, Hb, False, o_nsb[0])],
                  out_dram=o_yp)
        build_pos()
        loadH(Hf, h0f)
        loadH(Hb, h0b)
        chk("pos")
        prefix_all()
        full_tile(8, 1,
                  fchain=[(c, Hb, False, None) for c in range(4)],
                  bchain=[(c, Hf, False, None) for c in (3, 2, 1, 0)],
                  out_dram=o_ys[512:1024, :])
        full_tile(9, 1,
                  fchain=[(c, Hf0, False, None) for c in range(4)],
                  bchain=[(c, Hf, False, None) for c in (3, 2, 1, 0)],
                  out_dram=o_ys[0:512, :])
    except _Stop:
        pass
    if dumps:
        for nm in dumps:
            dump(nm, *DUMPABLE[nm])
    S.emit(final_wait_keys=["st"] if "st" in S.semkeys else [])
    es.close()
    return nc


_NC_CACHE = {}


def _host_inputs(inp):
    f32 = np.float32
    g = lambda k: np.asarray(inp[k], dtype=f32)
    x_prompt, x_sample = g("x_prompt"), g("x_sample")
    shared = {
        "w_ada": np.ascontiguousarray(g("w_ada")[0]),
        "w_in": np.ascontiguousarray(g("w_in")[0]),
        "w_pa": np.ascontiguousarray(g("w_proj_a")[0]),
        "w_pb": np.ascontiguousarray(g("w_proj_b")[0]),
        "w_out": np.ascontiguousarray(g("w_out")[0]),
        "w_ff1": np.ascontiguousarray(g("w_ff1")[0]),
        "w_ff2": np.ascontiguousarray(g("w_ff2")[0]),
        "b_ada_col": np.ascontiguousarray(g("b_ada")[0].reshape(48, 128).T),
        "b_ada_row": np.ascontiguousarray(g("b_ada")[0].reshape(1, -1)),
        "convw": np.ascontiguousarray(g("conv_w")[0].reshape(3, 12, 128).transpose(2, 1, 0).reshape(128, 36)),
        "convb": np.ascontiguousarray(g("conv_b")[0].reshape(12, 128).T),
        "dtb": np.concatenate([g("dt_bias_fwd")[0], g("dt_bias_bwd")[0]]).reshape(1, 32),
        "alog": np.concatenate([g("a_log_fwd")[0], g("a_log_bwd")[0]]).reshape(1, 32),
        "dskip": g("d_skip")[0].reshape(1, 16),
        "normw_col": np.ascontiguousarray(g("ssd_norm_w")[0].reshape(8, 128).T),
        "sgu_g": g("sgu_ln_g")[0].reshape(1, -1), "sgu_b": g("sgu_ln_b")[0].reshape(1, -1),
        "ln1_g": g("ln1_g")[0].reshape(1, -1), "ln1_b": g("ln1_b")[0].reshape(1, -1),
        "ln2_g": g("ln2_g")[0].reshape(1, -1), "ln2_b": g("ln2_b")[0].reshape(1, -1),
        "wspT": np.ascontiguousarray(g("w_spatial")[0].transpose(2, 0, 1).reshape(128, 1024)),
        "bsp": g("b_spatial")[0].reshape(1, 1024),
    }
    idx = np.arange(128)
    shared["ident"] = np.eye(128, dtype=f32)
    shared["m_le"] = (idx[:, None] <= idx[None, :]).astype(f32)
    shared["m_ge"] = (idx[:, None] >= idx[None, :]).astype(f32)
    shared["m_gt"] = (idx[:, None] > idx[None, :]).astype(f32)
    shared["m_lt"] = (idx[:, None] < idx[None, :]).astype(f32)
    shared["jidx"] = np.broadcast_to(np.arange(256, dtype=f32)[None, :], (64, 256)).copy()
    shared["pidx"] = np.arange(64, dtype=f32).reshape(64, 1)
    p64 = np.arange(64)
    shared["selcol"] = (p64[:, None] == (idx[None, :] % 64)).astype(f32)
    hcols = np.array([63, 0] * 4)
    shared["selcol_h"] = (p64[:, None] == hcols[None, :]).astype(f32)
    ecols = np.concatenate([[63], np.arange(128) % 64, [0]])
    shared["selc"] = (p64[:, None] == ecols[None, :]).astype(f32)
    zero_row = np.zeros((D,), f32)
    c_ctx, c = g("c_ctx"), g("c")
    sf_, sb_ = g("state_ssd_fwd"), g("state_ssd_bwd")
    in_maps = []
    for cid in range(8):
        b, q = cid // 4, cid % 4
        xall = np.zeros((NT, 520, D), f32)
        flags = np.zeros((NT, 4, 4), f32)
        selrow = np.zeros((NT - 1, 64, 512), f32)
        selrow_h = np.zeros((NT - 1, 64, 8), f32)
        for s in range(2):
            seq = x_prompt[2 * cid + s]
            for cc in range(2):
                ch = s * 2 + cc
                xall[0, ch * 128:(ch + 1) * 128] = seq[cc * 128:(cc + 1) * 128]
                if cc == 1:
                    xall[0, 512 + 2 * ch] = seq[127]
                    flags[0, ch, 0] = 1.0
                else:
                    xall[0, 512 + 2 * ch + 1] = seq[128]
                    flags[0, ch, 1] = 1.0
        xs = x_sample[b]
        slots = [(gc, 1.0, 0.0) for gc in range(0, 8 * q)] + [(gc, 0.0, 1.0) for gc in range(31, 8 * q + 7, -1)]
        assert len(slots) == 24
        slots += [(8 * q + j, 1.0, 0.0) for j in range(4)]
        slots += [(8 * q + 4 + j, 0.0, 0.0) for j in range(4)]
        slots += [(8 * q + j, 0.0, 0.0) for j in range(4)]
        for si, (gc, sf, sb) in enumerate(slots):
            ti, ch = 1 + si // 4, si % 4
            t0 = gc * 128
            xall[ti, ch * 128:(ch + 1) * 128] = xs[t0:t0 + 128]
            flags[ti, ch, 2], flags[ti, ch, 3] = sf, sb
            tok = t0 + idx
            selrow[ti - 1, tok // 64, ch * 128 + idx] = 1.0
            if t0 > 0:
                xall[ti, 512 + 2 * ch] = xs[t0 - 1]
                flags[ti, ch, 0] = 1.0
                selrow_h[ti - 1, (t0 - 1) // 64, 2 * ch] = 1.0
            if t0 + 128 < 4096:
                xall[ti, 512 + 2 * ch + 1] = xs[t0 + 128]
                flags[ti, ch, 1] = 1.0
                selrow_h[ti - 1, (t0 + 128) // 64, 2 * ch + 1] = 1.0
        ccol = np.stack([c_ctx.reshape(8, 128).T, c[b].reshape(8, 128).T], axis=-1).reshape(128, 16)
        ext = np.empty((NT, NCH, EXT, D), f32)
        ext[:, :, 1:129] = xall[:, :512].reshape(NT, NCH, 128, D)
        ext[:, :, 0] = xall[:, 512:520:2]
        ext[:, :, 129] = xall[:, 513:520:2]
        xTe = np.ascontiguousarray(ext.reshape(NT, NCH * EXT, D).transpose(0, 2, 1))
        selT = np.zeros((NT - 1, 64, NCH, EXT), f32)
        selT[:, :, :, 1:129] = selrow.reshape(NT - 1, 64, NCH, 128)
        selT[:, :, :, 0] = selrow_h[:, :, 0::2]
        selT[:, :, :, 129] = selrow_h[:, :, 1::2]
        m = dict(shared)
        m.update({
            "xres": np.ascontiguousarray(xall[[0, 8, 9], :512]),
            "xTe": xTe,
            "selT": selT.reshape(NT - 1, 64, NCH * EXT),
            "flags": np.ascontiguousarray(np.broadcast_to(flags.reshape(1, -1), (128, NT * 16))),
            "swf": np.ascontiguousarray(np.broadcast_to((np.arange(4) == q).astype(f32).reshape(1, 4), (128, 4))),
            "selrow": selrow, "selrow_h": selrow_h,
            "ccol": np.ascontiguousarray(ccol),
            "h0f": np.ascontiguousarray(sf_[b, 0].reshape(D, 128)),
            "h0b": np.ascontiguousarray(sb_[b, 0].reshape(D, 128)),
        })
        in_maps.append(m)
    return in_maps


def kernel(**inputs):
    if "nc" not in _NC_CACHE:
        _NC_CACHE["nc"] = build_program()
    nc = _NC_CACHE["nc"]
    in_maps = _host_inputs(inputs)
    res = run_bass_kernel_spmd(nc, in_maps, core_ids=list(range(8)))
    r = res.results
    y_prompt = np.zeros((16, 256, D), np.float32)
    y_sample = np.zeros((2, 4096, D), np.float32)
    nsf = np.zeros((16, 1, 16, 64, 128), np.float32)
    nsb = np.zeros((16, 1, 16, 64, 128), np.float32)
    for cid in range(8):
        b, q = cid // 4, cid % 4
        yp = np.asarray(r[cid]["yp"]).reshape(2, 256, D)
        y_prompt[2 * cid:2 * cid + 2] = yp
        y_sample[b, q * 1024:(q + 1) * 1024] = np.asarray(r[cid]["ys"])
        nsf[2 * cid:2 * cid + 2, 0] = np.asarray(r[cid]["nsf"]).reshape(2, 16, 64, 128)
        nsb[2 * cid:2 * cid + 2, 0] = np.asarray(r[cid]["nsb"]).reshape(2, 16, 64, 128)
    return (y_prompt, y_sample, nsf, nsb)
```
